# Optimizing a Trainium2 kernel written in Bass

```python
import jax, jax.numpy as jnp
from jax import lax
import numpy as np

D_MODEL = 1024
BATCH = 8
SEQ = 2048
DEPTH = 4

CHUNK = 64
EPS = 1e-6
N_HEADS = 4
HEAD_DIM = 64
MIX_W = N_HEADS * HEAD_DIM
N_BRANCH = 4
ATT_LEFT_CHUNKS = 8
ATT_BAND = (ATT_LEFT_CHUNKS + 1) * CHUNK
REL_MAX = 256
REL_SIZE = REL_MAX + CHUNK
NEG_BIG = -1e30
HG_BLOCK = 16
LOG_FLOOR = 1e-30
GM_BLOCK = 128
GM_GROUPS = N_HEADS
GM_GROUP_W = MIX_W // GM_GROUPS
CONV_W = 4
LRU_C = 8.0
FFN_HIDDEN = ((8 * D_MODEL // 3 + 255) // 256) * 256
IN_SIZES = [MIX_W] * 11 + [N_BRANCH * D_MODEL]
IN_COLS = sum(IN_SIZES)

kernel_name = 'hybrid_chunk_causal_parallel_mixer'


def rms_norm(x, g):
    xf = x.astype(jnp.float32)
    y = xf * lax.rsqrt(jnp.mean(xf * xf, axis=-1, keepdims=True) + EPS)
    return (y * g.astype(jnp.float32)).astype(x.dtype)


def split_in(z):
    idx = np.cumsum(IN_SIZES)[:-1].tolist()
    return jnp.split(z, idx, axis=-1)


def chunk_band_attention(q, k, v, rel_bias):
    B, S, _ = q.shape
    nc = S // CHUNK
    L = ATT_LEFT_CHUNKS
    pad = L * CHUNK
    qc = q.reshape(B, nc, CHUNK, N_HEADS, HEAD_DIM)

    def band(t):
        tp = jnp.pad(t, ((0, 0), (pad, 0), (0, 0))).reshape(B, nc + L, CHUNK, N_HEADS, HEAD_DIM)
        return jnp.concatenate([tp[:, j:j + nc] for j in range(L + 1)], axis=2)

    kb, vb = band(k), band(v)
    s = jnp.einsum('bcqhd,bckhd->bchqk', qc, kb).astype(jnp.float32) * (HEAD_DIM ** -0.5)
    dist = pad + jnp.arange(CHUNK)[:, None] - jnp.arange(ATT_BAND)[None, :]
    idx = jnp.clip(dist, -(CHUNK - 1), REL_MAX) + (CHUNK - 1)
    bias = rel_bias.astype(jnp.float32)[:, idx]
    key_pos = jnp.arange(nc)[:, None] * CHUNK + jnp.arange(ATT_BAND)[None, :] - pad
    valid = key_pos >= 0
    s = jnp.where(valid[None, :, None, None, :], s + bias[None, None], NEG_BIG)
    p = jax.nn.softmax(s, axis=-1).astype(v.dtype)
    o = jnp.einsum('bchqk,bckhd->bcqhd', p, vb)
    return o.reshape(B, S, MIX_W)


def hgrn2(q, fz, i, g, lb, norm_g):
    B, S, _ = q.shape
    n = S // HG_BLOCK
    f32 = jnp.float32
    fz = fz.astype(f32)
    lb = lb.astype(f32)
    qf = jax.nn.silu(q.astype(f32))
    f = lb + (1.0 - lb) * jax.nn.sigmoid(fz)
    log_f = jnp.log(jnp.maximum(f, LOG_FLOOR))
    kf = (1.0 - lb) * jax.nn.sigmoid(-fz)
    shp = (B, n, HG_BLOCK, N_HEADS, HEAD_DIM)
    qf, kf, log_f = qf.reshape(shp), kf.reshape(shp), log_f.reshape(shp)
    vf = i.astype(f32).reshape(shp)
    b = jnp.cumsum(log_f, axis=2)
    causal = jnp.tril(jnp.ones((HG_BLOCK, HG_BLOCK), bool))[:, :, None, None]
    diff = b[:, :, :, None] - b[:, :, None, :]
    decay = jnp.where(causal, jnp.exp(jnp.where(causal, diff, 0.0)), 0.0)
    scores = jnp.einsum('bntshk,bnthk,bnshk->bnhts', decay, qf, kf)
    intra = jnp.einsum('bnhts,bnshv->bnthv', scores, vf)
    b_last = b[:, :, -1:]
    qd = qf * jnp.exp(b)
    kd = kf * jnp.exp(b_last - b)
    dec = jnp.exp(b_last[:, :, 0])

    def step(state, xs):
        qd_c, kd_c, v_c, dec_c = xs
        inter_c = jnp.einsum('bthk,bhkv->bthv', qd_c, state)
        state = dec_c[..., None] * state + jnp.einsum('bshk,bshv->bhkv', kd_c, v_c)
        return state, inter_c

    s0 = jnp.zeros((B, N_HEADS, HEAD_DIM, HEAD_DIM), f32)
    xs = (jnp.moveaxis(qd, 1, 0), jnp.moveaxis(kd, 1, 0), jnp.moveaxis(vf, 1, 0), jnp.moveaxis(dec, 1, 0))
    _, inter = lax.scan(step, s0, xs)
    o = (intra + jnp.moveaxis(inter, 0, 1)).reshape(B, S, N_HEADS, HEAD_DIM)
    o = o * lax.rsqrt(jnp.mean(o * o, axis=-1, keepdims=True) + EPS)
    o = o * norm_g.astype(f32).reshape(N_HEADS, HEAD_DIM)
    o = o.reshape(B, S, MIX_W) * jax.nn.silu(g.astype(f32))
    return o.astype(q.dtype)


def spatial_gating(u, v, norm_g, ws, bs):
    B, S, _ = u.shape
    n = S // GM_BLOCK
    vn = rms_norm(v, norm_g).reshape(B, n, GM_BLOCK, GM_GROUPS, GM_GROUP_W)
    w = ws * jnp.tril(jnp.ones((GM_BLOCK, GM_BLOCK), ws.dtype))
    mixed = jnp.einsum('gpq,bnqgc->bnpgc', w, vn) + bs.T[:, :, None]
    return u * mixed.reshape(B, S, MIX_W)


def rg_lru_branch(xin, gate, conv_w, conv_b, wa, ba, wx, bx, lam):
    B, S, _ = xin.shape
    f32 = jnp.float32
    xp = jnp.pad(xin, ((0, 0), (CONV_W - 1, 0), (0, 0)))
    xc = conv_b + xp[:, 0:S] * conv_w[0]
    for j in range(1, CONV_W):
        xc = xc + xp[:, j:j + S] * conv_w[j]
    xh = xc.reshape(B, S, N_HEADS, HEAD_DIM)
    r = jax.nn.sigmoid(jnp.einsum('bshi,hij->bshj', xh, wa).reshape(B, S, MIX_W) + ba)
    ig = jax.nn.sigmoid(jnp.einsum('bshi,hij->bshj', xh, wx).reshape(B, S, MIX_W) + bx)
    log_a = -LRU_C * r.astype(f32) * jax.nn.softplus(-lam.astype(f32))
    a = jnp.exp(log_a)
    mult = jnp.sqrt(jnp.maximum(-jnp.expm1(2.0 * log_a), 0.0))
    mult = jnp.where(jnp.arange(S)[None, :, None] == 0, 1.0, mult)
    bt = mult * (ig * xc).astype(f32)

    def combine(left, right):
        a1, b1 = left
        a2, b2 = right
        return a1 * a2, a2 * b1 + b2

    _, h = lax.associative_scan(combine, (a, bt), axis=1)
    return (h * jax.nn.gelu(gate.astype(f32))).astype(xin.dtype)


def hybrid_mixer(h, w_in, rel_bias, lb, hg_norm_g, gm_norm_g, gm_ws, gm_bs,
                 conv_w, conv_b, wa, ba, wx, bx, lam, w_branch, w_out):
    B, S, _ = h.shape
    z = h @ w_in
    aq, ak, av, bq, bf, bi, bg, cu, cv, dx, dg, gates = split_in(z)
    o_a = chunk_band_attention(aq, ak, av, rel_bias)
    o_b = hgrn2(bq, bf, bi, bg, lb, hg_norm_g)
    o_c = spatial_gating(jax.nn.gelu(cu), jax.nn.gelu(cv), gm_norm_g, gm_ws, gm_bs)
    o_d = rg_lru_branch(dx, dg, conv_w, conv_b, wa, ba, wx, bx, lam)
    outs = jnp.stack([o_a, o_b, o_c, o_d], axis=2)
    proj = jnp.einsum('bsnw,nwd->bsnd', outs, w_branch)
    g = jax.nn.sigmoid(gates.reshape(B, S, N_BRANCH, D_MODEL))
    merged = jnp.sum(g * proj, axis=2)
    return merged @ w_out


def swiglu(h, w1, w2):
    gt, up = jnp.split(h @ w1, 2, axis=-1)
    return (jax.nn.silu(gt) * up) @ w2


def setup_inputs(seed: int = 0) -> dict:
    key = jax.random.key(seed)
    ks = jax.random.split(key, 24)
    f32 = jnp.float32

    def nrm(k, shape, scale):
        return jax.random.normal(k, shape, f32) * scale

    u = jax.random.uniform(ks[18], (DEPTH, MIX_W), f32, 0.9, 0.999)
    sa = u ** (1.0 / LRU_C)
    return {
        'x': nrm(ks[0], (BATCH, SEQ, D_MODEL), 1.0),
        'norm_mix_pre': 1.0 + nrm(ks[1], (DEPTH, D_MODEL), 0.02),
        'norm_mix_post': 1.0 + nrm(ks[2], (DEPTH, D_MODEL), 0.02),
        'norm_ffn_pre': 1.0 + nrm(ks[3], (DEPTH, D_MODEL), 0.02),
        'norm_ffn_post': 1.0 + nrm(ks[4], (DEPTH, D_MODEL), 0.02),
        'w_in': nrm(ks[5], (DEPTH, D_MODEL, IN_COLS), D_MODEL ** -0.5),
        'attn_rel_bias': nrm(ks[6], (DEPTH, N_HEADS, REL_SIZE), 0.1),
        'hgrn_lb_logits': nrm(ks[7], (DEPTH, MIX_W), 1.0),
        'hgrn_norm_g': 1.0 + nrm(ks[8], (DEPTH, MIX_W), 0.02),
        'gmlp_norm_g': 1.0 + nrm(ks[9], (DEPTH, MIX_W), 0.02),
        'gmlp_ws': nrm(ks[10], (DEPTH, GM_GROUPS, GM_BLOCK, GM_BLOCK), 0.5 * GM_BLOCK ** -0.5),
        'gmlp_bs': 1.0 + nrm(ks[11], (DEPTH, GM_GROUPS, GM_BLOCK), 0.01),
        'lru_conv_w': nrm(ks[12], (DEPTH, CONV_W, MIX_W), CONV_W ** -0.5),
        'lru_conv_b': nrm(ks[13], (DEPTH, MIX_W), 0.01),
        'lru_wa': nrm(ks[14], (DEPTH, N_HEADS, HEAD_DIM, HEAD_DIM), HEAD_DIM ** -0.5),
        'lru_ba': nrm(ks[15], (DEPTH, MIX_W), 0.01),
        'lru_wx': nrm(ks[16], (DEPTH, N_HEADS, HEAD_DIM, HEAD_DIM), HEAD_DIM ** -0.5),
        'lru_bx': nrm(ks[17], (DEPTH, MIX_W), 0.01),
        'lru_lambda': jnp.log(sa) - jnp.log1p(-sa),
        'w_branch': nrm(ks[19], (DEPTH, N_BRANCH, MIX_W, D_MODEL), MIX_W ** -0.5),
        'w_out': nrm(ks[20], (DEPTH, D_MODEL, D_MODEL), D_MODEL ** -0.5),
        'w_ffn_in': nrm(ks[21], (DEPTH, D_MODEL, 2 * FFN_HIDDEN), D_MODEL ** -0.5),
        'w_ffn_out': nrm(ks[22], (DEPTH, FFN_HIDDEN, D_MODEL), FFN_HIDDEN ** -0.5),
    }


def reference(x, norm_mix_pre, norm_mix_post, norm_ffn_pre, norm_ffn_post, w_in,
              attn_rel_bias, hgrn_lb_logits, hgrn_norm_g, gmlp_norm_g, gmlp_ws, gmlp_bs,
              lru_conv_w, lru_conv_b, lru_wa, lru_ba, lru_wx, lru_bx, lru_lambda,
              w_branch, w_out, w_ffn_in, w_ffn_out):
    p = jax.nn.softmax(hgrn_lb_logits.astype(jnp.float32), axis=0)
    lbs = jnp.cumsum(p, axis=0) - p[0]
    for l in range(DEPTH):
        h = rms_norm(x, norm_mix_pre[l])
        y = hybrid_mixer(h, w_in[l], attn_rel_bias[l], lbs[l], hgrn_norm_g[l], gmlp_norm_g[l],
                         gmlp_ws[l], gmlp_bs[l], lru_conv_w[l], lru_conv_b[l], lru_wa[l],
                         lru_ba[l], lru_wx[l], lru_bx[l], lru_lambda[l], w_branch[l], w_out[l])
        x = x + rms_norm(y, norm_mix_post[l])
        h = rms_norm(x, norm_ffn_pre[l])
        x = x + rms_norm(swiglu(h, w_ffn_in[l], w_ffn_out[l]), norm_ffn_post[l])
    return x
```

```python
import itertools
import numpy as np
import concourse.bass as bass
import concourse.mybir as mybir
from concourse.bass_utils import run_bass_kernel_spmd

F32 = mybir.dt.float32
BF16 = mybir.dt.bfloat16
AF = mybir.ActivationFunctionType
ALU = mybir.AluOpType
AX = mybir.AxisListType

D = 1024
NKC = 8
MIXW = 256
IN_COLS = 6912
FFH = 2816
NJ = 22
EPS = 1e-6
NEG = -30000.0
SLOT = 1024
NPCOL = 64


class Space:
    def __init__(self, n):
        self.lastw = [None] * n
        self.reads = [dict() for _ in range(n)]


class Acc:
    __slots__ = ("ap", "space", "slots")

    def __init__(self, ap, space, slots):
        self.ap = ap
        self.space = space
        self.slots = slots


class View:
    def __init__(self, S, name, fshape, dtype, off):
        self.S = S
        self.fshape = list(fshape)
        self.esz = mybir.dt.size(dtype)
        self.off = off
        self.nbytes = int(np.prod(fshape)) * self.esz
        assert off % 32 == 0 and off + self.nbytes <= S.arena_bytes, (name, off, self.nbytes)
        S.nview += 1
        self.t = S.nc.alloc_sbuf_tensor_at("%s_%d" % (name, S.nview), [128] + self.fshape, dtype,
                                           offset=S.arena_base + off)
        st = [1] * len(fshape)
        for i in range(len(fshape) - 2, -1, -1):
            st[i] = st[i + 1] * fshape[i + 1]
        self.strides = st

    def __getitem__(self, key):
        if not isinstance(key, tuple):
            key = (key,)
        fk = list(key[1:]) + [slice(None)] * (len(self.fshape) - (len(key) - 1))
        rng = []
        for k, n in zip(fk, self.fshape):
            if isinstance(k, int):
                rng.append((k, k + 1))
            else:
                lo = 0 if k.start is None else k.start
                hi = n if k.stop is None else k.stop
                rng.append((lo, hi))
        nd = len(rng)
        last = nd - 1
        while last > 0 and rng[last] == (0, self.fshape[last]):
            last -= 1
        slots = set()
        inner = self.strides[last]
        for idx in itertools.product(*[range(a, b) for a, b in rng[:last]]):
            base = sum(i * s for i, s in zip(idx, self.strides))
            b0 = self.off + (base + rng[last][0] * inner) * self.esz
            b1 = self.off + (base + rng[last][1] * inner) * self.esz
            slots.update(range(b0 // SLOT, (b1 - 1) // SLOT + 1))
        return Acc(self.t[key], self.S.arena, sorted(slots))


class PBank:
    def __init__(self, S, i):
        self.t = S.nc.alloc_psum_tensor("psb%d" % i, [128, 512], F32)
        self.space = Space(1)

    def __getitem__(self, key):
        return Acc(self.t[key], self.space, (0,))

    def bf(self, key):
        return Acc(self.t[:].bitcast(BF16)[key], self.space, (0,))


class Sched:
    ENG = ("pe", "dve", "act", "pool", "sp")

    def __init__(self, nc, arena_kib=206, n_dma_sems=6):
        self.nc = nc
        self.e = {"pe": nc.tensor, "dve": nc.vector, "act": nc.scalar,
                  "pool": nc.gpsimd, "sp": nc.sync}
        self.sem = {k: nc.alloc_semaphore("s_" + k) for k in self.ENG}
        self.cnt = {k: 0 for k in self.ENG}
        self.dsem, self.dcnt, self.drr = {}, {}, {}
        for q in ("sp", "pool"):
            self.dsem[q] = [nc.alloc_semaphore("d_%s%d" % (q, i)) for i in range(n_dma_sems)]
            self.dcnt[q] = [0] * n_dma_sems
            self.drr[q] = 0
        self.waited = {k: {} for k in self.ENG}
        self.out_waits = []
        self.nview = 0
        self.arena_bytes = arena_kib * 1024
        nc.alloc_sbuf_tensor("arena", [128, self.arena_bytes], mybir.dt.uint8)
        self.arena_base = list(nc.allocations)[-1].memorylocations[0].addr
        self.arena = Space(self.arena_bytes // SLOT)
        self.banks = [PBank(self, i) for i in range(8)]
        self.nops = 0

    def _semof(self, key):
        if isinstance(key, str):
            return self.sem[key]
        return self.dsem[key[0]][key[1]]

    def _wait(self, eng, key, val):
        w = self.waited[eng]
        if w.get(key, 0) >= val:
            return
        self.e[eng].wait_ge(self._semof(key), val)
        w[key] = val

    def _deps(self, eng, reads, writes):
        deps = {}
        for a in reads:
            sp = a.space
            for s in a.slots:
                d = sp.lastw[s]
                if d is not None and deps.get(d[0], 0) < d[1]:
                    deps[d[0]] = d[1]
        for a in writes:
            sp = a.space
            for s in a.slots:
                d = sp.lastw[s]
                if d is not None and deps.get(d[0], 0) < d[1]:
                    deps[d[0]] = d[1]
                for k, v in sp.reads[s].items():
                    if deps.get(k, 0) < v:
                        deps[k] = v
        for k, v in deps.items():
            if k == eng and eng == "pe":
                continue
            self._wait(eng, k, v)

    def _record(self, key, val, reads, writes):
        for a in reads:
            sp = a.space
            for s in a.slots:
                sp.reads[s][key] = val
        for a in writes:
            sp = a.space
            for s in a.slots:
                sp.lastw[s] = (key, val)
                sp.reads[s] = {}

    def op(self, eng, fn, reads=(), writes=()):
        self._deps(eng, reads, writes)
        ins = fn(self.e[eng])
        self.cnt[eng] += 1
        ins.then_inc(self.sem[eng], 1)
        self._record(eng, self.cnt[eng], reads, writes)
        self.nops += 1
        return ins

    def dma(self, q, out, in_, reads=(), writes=(), is_output=False):
        i = self.drr[q]
        self.drr[q] = (i + 1) % len(self.dsem[q])
        key = (q, i)
        if self.dcnt[q][i] > 0:
            self._wait(q, key, self.dcnt[q][i])
        self._deps(q, reads, writes)
        ins = self.e[q].dma_start(out=out, in_=in_)
        self.dcnt[q][i] += 16
        ins.then_inc(self.dsem[q][i], 16)
        self._record(key, self.dcnt[q][i], reads, writes)
        if is_output:
            self.out_waits.append((key, self.dcnt[q][i]))
        self.nops += 1
        return ins

    def finish(self):
        for key, v in self.out_waits:
            self._wait("sp", key, v)
        for k in ("pe", "dve", "act", "pool"):
            if self.cnt[k]:
                self._wait("sp", k, self.cnt[k])

    def mm(self, out, lhsT, rhs, start=True, stop=True, skip=False):
        return self.op("pe", lambda e: e.matmul(out.ap, lhsT=lhsT.ap, rhs=rhs.ap, start=start, stop=stop,
                                                skip_group_check=skip),
                       reads=[lhsT, rhs], writes=[out])

    def transpose(self, out, in_, ident):
        return self.op("pe", lambda e: e.transpose(out.ap, in_.ap, ident.ap), reads=[in_, ident], writes=[out])

    def act(self, out, in_, func, scale=None, bias=None, accum=None, eng="act"):
        kw = {}
        rd = [in_]
        wr = [out]
        if scale is not None:
            if isinstance(scale, Acc):
                rd.append(scale)
                kw["scale"] = scale.ap
            else:
                kw["scale"] = scale
        if bias is not None:
            if isinstance(bias, Acc):
                rd.append(bias)
                kw["bias"] = bias.ap
            else:
                kw["bias"] = bias
        if accum is not None:
            wr.append(accum)
            kw["accum_out"] = accum.ap
        return self.op(eng, lambda e: e.activation(out=out.ap, in_=in_.ap, func=func, **kw), reads=rd, writes=wr)

    def tt(self, eng, out, in0, in1, op):
        return self.op(eng, lambda e: e.tensor_tensor(out=out.ap, in0=in0.ap, in1=in1.ap, op=op),
                       reads=[in0, in1], writes=[out])

    def ts(self, eng, out, in0, s1, s2, op0, op1=None):
        rd = [in0]
        a1 = s1.ap if isinstance(s1, Acc) else s1
        a2 = s2.ap if isinstance(s2, Acc) else s2
        if isinstance(s1, Acc):
            rd.append(s1)
        if isinstance(s2, Acc):
            rd.append(s2)
        if op1 is None:
            return self.op(eng, lambda e: e.tensor_scalar(out=out.ap, in0=in0.ap, scalar1=a1, scalar2=None, op0=op0),
                           reads=rd, writes=[out])
        return self.op(eng, lambda e: e.tensor_scalar(out=out.ap, in0=in0.ap, scalar1=a1, scalar2=a2, op0=op0, op1=op1),
                       reads=rd, writes=[out])

    def stt(self, eng, out, in0, scalar, in1, op0, op1):
        rd = [in0, in1]
        sc = scalar.ap if isinstance(scalar, Acc) else scalar
        if isinstance(scalar, Acc):
            rd.append(scalar)
        return self.op(eng, lambda e: e.scalar_tensor_tensor(out=out.ap, in0=in0.ap, scalar=sc, in1=in1.ap, op0=op0, op1=op1),
                       reads=rd, writes=[out])

    def copy(self, eng, out, in_):
        if eng == "act":
            return self.act(out, in_, AF.Copy)
        return self.op(eng, lambda e: e.tensor_copy(out=out.ap, in_=in_.ap), reads=[in_], writes=[out])

    def memset(self, eng, out, val):
        return self.op(eng, lambda e: e.memset(out.ap, val), writes=[out])


def bcast(acc, shape):
    return Acc(acc.ap.unsqueeze(len(acc.ap.shape)).to_broadcast(list(shape)), acc.space, acc.slots)


class Alloc:
    def __init__(self, S, lo, hi):
        self.S, self.lo, self.hi, self.cur = S, lo, hi, lo

    def __call__(self, name, fshape, dtype, own=False):
        n = int(np.prod(fshape)) * mybir.dt.size(dtype)
        al = SLOT if (n >= SLOT or own) else 64
        off = (self.cur + al - 1) // al * al
        assert off + n <= self.hi, ("arena region overflow", name, off, n, self.hi)
        self.cur = off + n
        if n >= SLOT or own:
            self.cur = (self.cur + SLOT - 1) // SLOT * SLOT
        return View(self.S, name, fshape, dtype, off)


class WStream:
    NSLOT = 8
    SLOT_EL = 2048
    AHEAD = 5

    def __init__(self, S, alloc, plan):
        self.S = S
        self.plan = plan
        self.slots = [alloc("wslot%d" % i, [self.SLOT_EL], BF16) for i in range(self.NSLOT)]
        self.views = {}
        self.issued = 0
        self.k = 0

    def _view(self, si, a, b):
        key = (si, a, b)
        if key not in self.views:
            sl = self.slots[si]
            self.views[key] = View(self.S, "wv", [a, b], BF16, sl.off)
        return self.views[key]

    def _issue(self):
        tag, src, (a, b) = self.plan[self.issued]
        v = self._view(self.issued % self.NSLOT, a, b)
        self.S.dma("pool", v[:].ap, src, writes=[v[:]])
        self.issued += 1

    def next(self, tag):
        k = self.k
        assert self.plan[k][0] == tag, (k, self.plan[k][0], tag)
        while self.issued < min(len(self.plan), k + self.AHEAD + 1):
            self._issue()
        self.k += 1
        _, _, (a, b) = self.plan[k]
        return self._view(k % self.NSLOT, a, b)


def build_program(SEQ, DEPTH, dbg=False, stop=None):
    NT = SEQ // 128
    TT = SEQ // 512
    NCH = SEQ // 64
    nc = bass.Bass("TRN2", target_bir_lowering=False)
    S = Sched(nc)
    PS = S.banks

    def dram(name, shape, dt=F32, kind="ExternalInput"):
        return nc.dram_tensor(name, list(shape), dt, kind=kind).ap()

    x_in = dram("x", [SEQ, D])
    w_in = dram("w_in", [DEPTH, D, IN_COLS])
    w_br = dram("w_branch", [DEPTH, 4, MIXW, D])
    w_out = dram("w_out", [DEPTH, D, D])
    w_f1 = dram("w_ffn_in", [DEPTH, D, 2 * FFH])
    w_f2 = dram("w_ffn_out", [DEPTH, FFH, D])
    pcol_d = dram("pcol", [DEPTH, 128, NPCOL])
    gnorm_d = dram("gnorm_bc", [DEPTH, 128, MIXW])
    wsT_d = dram("wsT", [DEPTH, 128, 4, 128])
    bst_d = dram("bs_t", [DEPTH, 128, 2, 128])
    wax_d = dram("wax", [DEPTH, 128, 2, 2, 128])
    abias_d = dram("abias", [DEPTH, 128, 5, 4, 128])
    cmask_d = dram("cmasks", [128, 4, 512])
    out_d = dram("out", [SEQ, D], kind="ExternalOutput")
    xT_d = dram("xT_scratch", [128, NKC, SEQ], kind="Internal")
    xT_space = Space(TT)
    if dbg:
        dbg_d = dram("dbg_oall", [128, 4, 2, SEQ], BF16, kind="ExternalOutput")

    KB = 1024
    A_const = Alloc(S, 0, 24 * KB)
    A_w = Alloc(S, 24 * KB, 56 * KB)
    A_h = Alloc(S, 56 * KB, 56 * KB + 16 * SEQ)
    p0 = 56 * KB + 16 * SEQ
    A_o = Alloc(S, p0, p0 + 16 * SEQ)
    p1 = p0 + 16 * SEQ
    ARENA_END = S.arena_bytes

    hT = A_h("hT", [NKC, SEQ], BF16)
    o_all = A_o("o_all", [4, 2, SEQ], BF16)

    ident = A_const("ident", [128], BF16)
    ones_bf = A_const("ones", [128], BF16)
    bones = A_const("bones", [128], BF16)
    cm = A_const("cmasks", [4, 512], F32)
    pcol = A_const("pcol", [DEPTH, NPCOL], F32)
    lbt = A_const("lbt", [DEPTH, 2, 2], F32)
    lruc = A_const("lruc", [DEPTH, 2, 2], F32)
    tmpc = A_const("tmpc", [32], F32)
    gnorm = A_const("gnorm", [MIXW], F32)
    wsTf = A_const("wsTf", [4, 128], F32)
    wsTb = A_const("wsTb", [4, 128], BF16)
    bst = A_const("bst", [2, 128], F32)
    waxf = A_const("waxf", [2, 2, 128], F32)
    waxb = A_const("waxb", [2, 2, 128], BF16)
    decs = A_const("decs", [2, NCH], F32)
    rcol = A_const("rcol", [16], F32)

    def wsrc(w2d, c0, ncols, k0=0, nk=NKC):
        return w2d[k0 * 128:(k0 + nk) * 128, c0:c0 + ncols].rearrange("(kc p) c -> p kc c", p=128)

    plan = []
    for l in range(DEPTH):
        wi = w_in[l]
        plan.append(("Aq", wsrc(wi, 0, 256), (NKC, 256)))
        plan.append(("Ak", wsrc(wi, 256, 256), (NKC, 256)))
        plan.append(("Av", wsrc(wi, 512, 256), (NKC, 256)))
        plan.append(("Bf", wsrc(wi, 1024, 256), (NKC, 256)))
        plan.append(("Bq", wsrc(wi, 768, 256), (NKC, 256)))
        plan.append(("Bi", wsrc(wi, 1280, 256), (NKC, 256)))
        plan.append(("Bg", wsrc(wi, 1536, 256), (NKC, 256)))
        plan.append(("Cv", wsrc(wi, 2048, 256), (NKC, 256)))
        plan.append(("Cu", wsrc(wi, 1792, 256), (NKC, 256)))
        plan.append(("Dg", wsrc(wi, 2560, 256), (NKC, 256)))
        plan.append(("Dx", wsrc(wi, 2304, 256), (NKC, 256)))
        for tb in range(TT):
            for n in range(4):
                for q4 in range(4):
                    plan.append(("G", wsrc(wi, 2816 + n * 1024 + q4 * 256, 256), (NKC, 256)))
                    plan.append(("WB", wsrc(w_br[l, n], q4 * 256, 256, 0, 2), (2, 256)))
            for q4 in range(4):
                plan.append(("WO", wsrc(w_out[l], q4 * 256, 256), (NKC, 256)))
            for j2 in range(NJ // 2):
                plan.append(("F1g", wsrc(w_f1[l], j2 * 256, 256), (NKC, 256)))
                plan.append(("F1u", wsrc(w_f1[l], FFH + j2 * 256, 256), (NKC, 256)))
            for dc in range(NKC):
                for kh in range(2):
                    plan.append(("F2", wsrc(w_f2[l], dc * 128, 128, kh * 11, 11), (11, 128)))
    WS = WStream(S, A_w, plan)

    S.dma("sp", cm[:].ap, cmask_d, writes=[cm[:]])
    S.dma("sp", pcol[:].ap, pcol_d.rearrange("l p c -> p l c"), writes=[pcol[:]])
    S.memset("dve", ones_bf[:], 1.0)
    S.memset("dve", bones[:], 0.0)
    S.memset("dve", bones[0:64, 0:64], 1.0)
    S.memset("dve", bones[64:128, 64:128], 1.0)
    class _IdF:
        def __getitem__(self, key):
            return cm[:, 3, 0:128]
    identf = _IdF()
    S.copy("dve", ident[:], identf[:])
    lg = pcol[:, 0, 50:58]
    S.act(tmpc[:, 0:8], lg, AF.Exp)
    e3 = Acc(tmpc[:, 0:8].ap.rearrange("p (a d) -> p a d", a=2), S.arena, tmpc[:, 0:8].slots)
    S.op("dve", lambda e: e.reduce_sum(out=tmpc[:, 8:10].ap, in_=e3.ap, axis=AX.X), reads=[e3], writes=[tmpc[:, 8:10]])
    S.op("dve", lambda e: e.reciprocal(out=tmpc[:, 10:12].ap, in_=tmpc[:, 8:10].ap), reads=[tmpc[:, 8:10]], writes=[tmpc[:, 10:12]])
    for pr in range(2):
        S.ts("dve", tmpc[:, 12 + pr * 4:16 + pr * 4], tmpc[:, pr * 4:pr * 4 + 4], tmpc[:, 10 + pr:11 + pr], None, ALU.mult)
    for l in range(DEPTH):
        for pr in range(2):
            if l == 0:
                S.memset("dve", lbt[:, l, pr, 0:1], 0.0)
            else:
                S.tt("dve", lbt[:, l, pr, 0:1], lbt[:, l - 1, pr, 0:1], tmpc[:, 12 + pr * 4 + l:13 + pr * 4 + l], ALU.add)
            S.ts("dve", lbt[:, l, pr, 1:2], lbt[:, l, pr, 0:1], -1.0, 1.0, ALU.mult, ALU.add)
    for l in range(DEPTH):
        S.act(tmpc[:, 20:22], pcol[:, l, 48:50], AF.Exp, scale=-1.0)
        S.act(tmpc[:, 22:24], tmpc[:, 20:22], AF.Ln, bias=1.0)
        for pr in range(2):
            S.ts("dve", lruc[:, l, pr, 0:1], tmpc[:, 22 + pr:23 + pr], -8.0, None, ALU.mult)
            S.ts("dve", lruc[:, l, pr, 1:2], tmpc[:, 22 + pr:23 + pr], -16.0, None, ALU.mult)

    zring = [0]

    def zbank():
        b = PS[zring[0] % 3]
        zring[0] += 1
        return b

    def fm_matmul(bank, wv, c0, rhs_fn, nk=NKC, k0=0, first=True, last=True, ncols=128):
        for kc in range(nk):
            S.mm(bank, wv[:, kc, c0:c0 + ncols], rhs_fn(k0 + kc),
                 start=(first and kc == 0), stop=(last and kc == nk - 1))

    def rstd_from_ss(out, ss_bank, n, tmp):
        S.act(tmp, ss_bank, AF.Ln, scale=1.0 / n, bias=EPS)
        S.act(out, tmp, AF.Exp, scale=-0.5)

    def norm_to_hT(xt, gcol0, l, tb, A):
        sq = A("nsq", [NKC, 512], BF16)
        lnv = A("nln", [512], F32)
        rs = A("nrs", [512], F32)
        bank = PS[7]
        for c in range(NKC):
            S.tt("pool", sq[:, c, :], xt[:, c, :], xt[:, c, :], ALU.mult)
        for c in range(NKC):
            S.mm(bank[:, :], ones_bf[:], sq[:, c, :], start=(c == 0), stop=(c == NKC - 1))
        rstd_from_ss(rs[:], bank[:, :], D, lnv[:])
        for c in range(NKC):
            S.stt("dve", hT[:, c, tb * 512:(tb + 1) * 512], xt[:, c, :], pcol[:, l, gcol0 + c:gcol0 + c + 1], rs[:],
                  ALU.mult, ALU.mult)

    class Evac:
        def __init__(self, yT, gcol0, l, A):
            self.yT, self.g0, self.l = yT, gcol0, l
            self.sq = A("esq", [NKC, 512], BF16)
            self.pending = None

        def _ss(self, c):
            S.mm(PS[7][:, :], ones_bf[:], self.sq[:, c, :], start=(c == 0), stop=(c == NKC - 1))

        def chunk(self, dc, bank):
            S.act(self.yT[:, dc, :], bank[:, :], AF.Copy, scale=pcol[:, self.l, self.g0 + dc:self.g0 + dc + 1])
            S.act(self.sq[:, dc, :], bank[:, :], AF.Square)
            if self.pending is not None:
                self._ss(self.pending)
            self.pending = dc

        def finish(self):
            self._ss(self.pending)

    def resid_norm(xt, yT, l_next, gcol_next, tb, A, sq):
        lnv = A("rln", [512], F32)
        rs = A("rrs", [512], F32)
        rstd_from_ss(rs[:], PS[7][:, :], D, lnv[:])
        for c in range(NKC):
            S.tt("dve", yT[:, c, :], yT[:, c, :], rs[:], ALU.mult)
            S.tt("dve", xt[:, c, :], xt[:, c, :], yT[:, c, :], ALU.add)
            if l_next is not None:
                S.act(sq[:, c, :], xt[:, c, :], AF.Square)
        if l_next is None:
            return
        bank = PS[6]
        for c in range(NKC):
            S.mm(bank[:, :], ones_bf[:], sq[:, c, :], start=(c == 0), stop=(c == NKC - 1))
        lnv2 = A("rln2", [512], F32)
        rs2 = A("rrs2", [512], F32)
        rstd_from_ss(rs2[:], bank[:, :], D, lnv2[:])
        for c in range(NKC):
            S.stt("dve", hT[:, c, tb * 512:(tb + 1) * 512], xt[:, c, :], pcol[:, l_next, gcol_next + c:gcol_next + c + 1], rs2[:],
                  ALU.mult, ALU.mult)

    def load_input():
        A = Alloc(S, p1, ARENA_END)
        xtok = [A("xtok%d" % i, [D], F32) for i in range(2)]
        xts = [A("xt%d" % i, [NKC, 512], F32) for i in range(2)]
        for tb in range(TT):
            xt = xts[tb % 2]
            for i4 in range(4):
                i = tb * 4 + i4
                xk = xtok[i % 2]
                S.dma("sp", xk[:].ap, x_in[i * 128:(i + 1) * 128, :], writes=[xk[:]])
                for half in range(2):
                    bank = PS[3 + (2 * i + half) % 2]
                    for c4 in range(4):
                        c = half * 4 + c4
                        S.transpose(bank[:, c4 * 128:(c4 + 1) * 128], xk[:, c * 128:(c + 1) * 128], identf[:])
                    src = Acc(bank.t[:].rearrange("p (c t) -> p c t", c=4), bank.space, (0,))
                    S.copy("act" if half else "dve", xt[:, half * 4:(half + 1) * 4, i4 * 128:(i4 + 1) * 128], src)
            S.dma("sp", xT_d[:, :, tb * 512:(tb + 1) * 512], xt[:].ap, reads=[xt[:]],
                  writes=[Acc(None, xT_space, (tb,))])
            A2 = Alloc(S, A.cur, ARENA_END)
            norm_to_hT(xt, 0, 0, tb, A2)

    def mixer_A(l):
        A = Alloc(S, p1, ARENA_END)
        qT = A("qTm", [4, SEQ], BF16)
        kT = A("kT", [2, SEQ], BF16)
        vaug = A("vaug", [NT, 4, 65], BF16)
        ab = A("abias", [5, 4, 128], F32)
        SK = 3
        tmp = [A("atmp%d" % i, [512], F32) for i in range(SK + 1)]
        PT = [A("aPT%d" % i, [512], BF16) for i in range(SK + 1)]
        otok = [A("aotok%d" % i, [256], BF16, own=True) for i in range(2)]
        rec = [A("arec%d" % i, [4], F32, own=True) for i in range(2)]
        S.dma("sp", ab[:].ap, abias_d[l], writes=[ab[:]])
        S.memset("dve", vaug[:], 1.0)
        S.memset("pool", qT[:], 0.0)
        for c in range(4):
            if c % 2 == 0:
                wqk = WS.next("Aq" if c == 0 else "Ak")
            for tb in range(TT):
                bank = zbank()
                fm_matmul(bank[:, :], wqk, (c % 2) * 128, lambda kc: hT[:, kc, tb * 512:(tb + 1) * 512])
                if c < 2:
                    S.act(qT[0:64, 2 * c, tb * 512:(tb + 1) * 512], bank[0:64, :], AF.Copy, scale=0.125)
                    S.act(qT[64:128, 2 * c + 1, tb * 512:(tb + 1) * 512], bank[64:128, :], AF.Copy, scale=0.125)
                else:
                    S.copy("dve", kT[:, c - 2, tb * 512:(tb + 1) * 512], bank[:, :])
        if stop == "A1":
            return
        wv = WS.next("Av")
        for i in range(NT):
            bank = zbank()
            for kc in range(NKC):
                S.mm(bank[:, 0:256], hT[:, kc, i * 128:(i + 1) * 128], wv[:, kc, :], start=(kc == 0), stop=(kc == NKC - 1))
            src = Acc(bank.t[:, 0:256].rearrange("p (h d) -> p h d", h=4), bank.space, (0,))
            S.copy("act" if i % 2 else "dve", vaug[:, i, :, 0:64], src)
        if stop == "A2":
            return
        steps = [(m, kt) for m in range(NT) for kt in range(max(0, m - 4), m + 1)]

        def qk(si):
            m, kt = steps[si]
            bank = PS[si % (SK + 1)]
            for h in range(4):
                S.mm(bank[:, h * 128:(h + 1) * 128], kT[:, h // 2, kt * 128:(kt + 1) * 128],
                     qT[:, h, m * 128:(m + 1) * 128])
            j = kt - m + 4
            bsrc = Acc(ab.t[:, j].rearrange("p h q -> p (h q)"), S.arena, ab[:, j].slots)
            S.tt("dve", tmp[si % (SK + 1)][:], bank[:, :], bsrc, ALU.add)
            S.act(PT[si % (SK + 1)][:], tmp[si % (SK + 1)][:], AF.Exp)

        def pv(si):
            m, kt = steps[si]
            first = kt == max(0, m - 4)
            last = kt == m
            ob = PS[4 + m % 2]
            for h in range(4):
                S.mm(ob[:, h * 65:(h + 1) * 65], PT[si % (SK + 1)][:, h * 128:(h + 1) * 128], vaug[:, kt, h, :],
                     start=(first and h == 0), stop=last, skip=True)
            if last:
                o3 = Acc(ob.t[:, 0:260].rearrange("p (h d) -> p h d", h=4), ob.space, (0,))
                rc = rec[m % 2]
                S.op("dve", lambda e: e.reciprocal(out=rc[:].ap, in_=o3.ap[:, :, 64]), reads=[o3], writes=[rc[:]])
                ot = otok[m % 2]
                o3v = Acc(o3.ap[:, :, 0:64], ob.space, (0,))
                otv = Acc(ot.t[:].rearrange("p (h d) -> p h d", h=4), S.arena, ot[:].slots)
                S.tt("dve", otv, o3v, bcast(rc[:], [128, 4, 64]), ALU.mult)
                tb_ = PS[6 + m % 2]
                for pr in range(2):
                    S.transpose(tb_.bf((slice(None), slice(pr * 128, (pr + 1) * 128))), ot[:, pr * 128:(pr + 1) * 128], ident[:])
                srcT = Acc(tb_.t[:].bitcast(BF16)[:, 0:256].rearrange("p (a t) -> p a t", a=2), tb_.space, (0,))
                S.copy("act", o_all[:, 0, :, m * 128:(m + 1) * 128], srcT)

        for si in range(len(steps) + SK):
            if si < len(steps):
                qk(si)
            if si >= SK:
                pv(si - SK)

    def mixer_B(l):
        A = Alloc(S, p1, ARENA_END)
        qdT = A("qdm", [4, SEQ], BF16)
        kdT = A("kdT", [2, SEQ], BF16)
        sgate = A("sgate", [2, SEQ], BF16)
        vtm = A("vtm", [NT, 2, 256], BF16)
        kdtok = A("kdtok", [NT, 256], BF16)
        t2s = [[A("bt%d_%d" % (k, i), [512], F32) for i in range(5)] for k in range(1)] * 2
        t2s = [[x[0], x[1], x[4], x[3], x[4], x[2]] for x in t2s]
        t = t2s[0]
        Ust = [[A("Ust%d_%d" % (p_, i), [64], F32, own=True) for i in range(2)] for p_ in range(2)]
        Sbf = [[A("Sbf%d_%d" % (p_, i), [64], BF16, own=True) for i in range(5)] for p_ in range(2)]
        Am = [A("bAm%d" % i, [512], BF16) for i in range(2)]
        oraw = [A("boraw%d" % i, [256], F32) for i in range(2)]
        osq = [A("bosq%d" % i, [256], BF16) for i in range(2)]
        zeros = A("bzeros", [128], BF16)
        S.memset("dve", zeros[:], 0.0)
        S.memset("pool", qdT[:], 0.0)
        S.memset("pool", vtm[:], 0.0)
        wf = WS.next("Bf")
        wq = WS.next("Bq")
        its = [(pr, tb) for pr in range(2) for tb in range(TT)]

        def prep1(k):
            pr, tb = its[k]
            tok = slice(tb * 512, (tb + 1) * 512)
            sg, ff, bb, eb, enb, qf = t2s[k % 2]
            bf_ = zbank()
            fm_matmul(bf_[:, :], wf, pr * 128, lambda kc: hT[:, kc, tok])
            bq_ = zbank()
            fm_matmul(bq_[:, :], wq, pr * 128, lambda kc: hT[:, kc, tok])
            S.act(sg[:], bf_[:, :], AF.Sigmoid)
            S.act(qf[:], bq_[:, :], AF.Silu)
            S.ts("dve", ff[:], sg[:], lbt[:, l, pr, 1:2], lbt[:, l, pr, 0:1], ALU.mult, ALU.add)

        def prep2(k):
            pr, tb = its[k]
            tok = slice(tb * 512, (tb + 1) * 512)
            sg, ff, bb, eb, enb, qf = t2s[k % 2]
            S.act(bb[:], ff[:], AF.Ln)
            S.op("dve", lambda e: e.tensor_tensor_scan(out=sg[:].ap, data0=cm[:, 1, :].ap, data1=bb[:].ap, initial=0.0,
                                                       op0=ALU.mult, op1=ALU.add),
                 reads=[cm[:, 1, :], bb[:]], writes=[sg[:]])
            S.ts("dve", sg[:], sg[:], -80.0, None, ALU.max)
            S.act(eb[:], sg[:], AF.Exp)
            S.act(enb[:], sg[:], AF.Exp, scale=-1.0)
            S.tt("dve", qdT[0:64, 2 * pr, tok], qf[0:64, :], eb[0:64, :], ALU.mult)
            S.tt("dve", qdT[64:128, 2 * pr + 1, tok], qf[64:128, :], eb[64:128, :], ALU.mult)
            S.ts("pool", ff[:], ff[:], -1.0, 1.0, ALU.mult, ALU.add)
            S.tt("dve", kdT[:, pr, tok], ff[:], enb[:], ALU.mult)
            ebs = Acc(eb.t[:, 63:512:64], S.arena, eb[:].slots)
            S.copy("dve", decs[:, pr, tb * 8:(tb + 1) * 8], ebs)

        for k in range(len(its)):
            prep1(k)
            prep2(k)
        wi_ = WS.next("Bi")
        for i in range(NT):
            bank = zbank()
            for kc in range(NKC):
                S.mm(bank[:, 0:256], hT[:, kc, i * 128:(i + 1) * 128], wi_[:, kc, :], start=(kc == 0), stop=(kc == NKC - 1))
            S.copy("act", vtm[0:64, i, 0, :], bank[0:64, 0:256])
            S.copy("dve", vtm[64:128, i, 1, :], bank[64:128, 0:256])
        wg = WS.next("Bg")
        for pr in range(2):
            for tb in range(TT):
                tok = slice(tb * 512, (tb + 1) * 512)
                bank = zbank()
                fm_matmul(bank[:, :], wg, pr * 128, lambda kc: hT[:, kc, tok])
                S.act(sgate[:, pr, tok], bank[:, :], AF.Silu)
        for i in range(NT):
            tb_ = PS[3 + i % 2]
            for pr in range(2):
                S.transpose(tb_.bf((slice(None), slice(pr * 128, (pr + 1) * 128))), kdT[:, pr, i * 128:(i + 1) * 128], ident[:])
            S.copy("act" if i % 2 else "dve", kdtok[:, i, :], tb_.bf((slice(None), slice(0, 256))))
        def coreA(i):
            tok = slice(i * 128, (i + 1) * 128)
            sb = PS[i % 2]
            for h in range(4):
                S.mm(sb[:, h * 128:(h + 1) * 128], kdT[:, h // 2, tok], qdT[:, h, tok])
            db = PS[4 + i % 2]
            for cc in range(2):
                for h in range(4):
                    off = 64 * (h % 2)
                    pr = h // 2
                    S.mm(db[off:off + 64, (cc * 2 + pr) * 64:(cc * 2 + pr + 1) * 64],
                         kdtok[:, i, h * 64:(h + 1) * 64],
                         vtm[:, i, cc, h * 64:(h + 1) * 64])
            am = Am[i % 2]
            S.tt("dve", am[:], sb[:, :], cm[:, 0, :], ALU.mult)
            for cc in range(2):
                n = 2 * i + cc
                for pr in range(2):
                    dl = db[:, (cc * 2 + pr) * 64:(cc * 2 + pr + 1) * 64]
                    if n == 0:
                        S.copy("dve", Ust[pr][n % 2][:], dl)
                    else:
                        S.stt("dve", Ust[pr][n % 2][:], Ust[pr][(n + 1) % 2][:], decs[:, pr, n - 1:n], dl, ALU.mult, ALU.add)
                    if n + 1 < NCH:
                        S.act(Sbf[pr][(n + 1) % 5][:], Ust[pr][n % 2][:], AF.Copy, scale=decs[:, pr, n:n + 1])

        def coreB(i):
            tok = slice(i * 128, (i + 1) * 128)
            am = Am[i % 2]
            ob = PS[2 + i % 2]
            S.mm(ob[:, 0:256], zeros[:], am[:, 0:256], start=True, stop=False, skip=True)
            for h in range(4):
                off = 64 * (h % 2)
                pr = h // 2
                for cc in range(2):
                    S.mm(ob[off:off + 64, pr * 128:(pr + 1) * 128], vtm[:, i, cc, h * 64:(h + 1) * 64], am[:, h * 128:(h + 1) * 128],
                         start=False, stop=False, skip=True)
            for cc in range(2):
                n = 2 * i + cc
                if n == 0:
                    continue
                for h in range(4):
                    off = 64 * (h % 2)
                    pr = h // 2
                    S.mm(ob[off:off + 64, pr * 128 + cc * 64:pr * 128 + (cc + 1) * 64],
                         Sbf[pr][n % 5][:],
                         qdT[:, h, i * 128 + cc * 64:i * 128 + (cc + 1) * 64], start=False, stop=False, skip=True)
            orw = oraw[i % 2]
            S.copy("act", orw[:], ob[:, 0:256])
            S.tt("pool", osq[i % 2][:], orw[:], orw[:], ALU.mult)
            nb = PS[6 + i % 2]
            S.mm(nb[:, 0:256], bones[:], osq[i % 2][:])
            lnv = t[0]
            rs = t[1]
            rstd_from_ss(rs[:, 0:256], nb[:, 0:256], 64, lnv[:, 0:256])
            for pr in range(2):
                S.stt("dve", orw[:, pr * 128:(pr + 1) * 128], orw[:, pr * 128:(pr + 1) * 128], pcol[:, l, 32 + pr:33 + pr],
                      rs[:, pr * 128:(pr + 1) * 128], ALU.mult, ALU.mult)
            o2 = Acc(orw.t[:].rearrange("p (a t) -> p a t", a=2), S.arena, orw[:].slots)
            S.tt("dve", o_all[:, 1, :, tok], o2, sgate[:, :, tok], ALU.mult)

        for i in range(NT + 1):
            if i < NT:
                coreA(i)
            if i >= 1:
                coreB(i - 1)

    def mixer_C(l):
        A = Alloc(S, p1, ARENA_END)
        uT = A("uT", [2, SEQ], BF16)
        vntok = A("vntok", [NT, 256], BF16)
        junk = A("cjunk", [256], F32)
        t2 = [A("ct%d" % i, [256], F32) for i in range(2)]
        S.dma("sp", gnorm[:].ap, gnorm_d[l], writes=[gnorm[:]])
        S.dma("sp", wsTf[:].ap, wsT_d[l], writes=[wsTf[:]])
        S.dma("sp", bst[:].ap, bst_d[l], writes=[bst[:]])
        S.tt("dve", wsTb[:], wsTf[:], Acc(cm.t[:, 2, 0:128].unsqueeze(1).to_broadcast([128, 4, 128]), S.arena, cm[:, 2, 0:128].slots),
             ALU.mult)
        wv = WS.next("Cv")
        vall = A("cvall", [NT, 256], F32)
        css = A("css", [3, NT], F32)
        S.memset("dve", css[:], 0.0)
        for i in range(NT):
            bank = zbank()
            for kc in range(NKC):
                S.mm(bank[:, 0:256], hT[:, kc, i * 128:(i + 1) * 128], wv[:, kc, :], start=(kc == 0), stop=(kc == NKC - 1))
            S.act(vall[:, i, :], bank[:, 0:256], AF.Gelu_apprx_tanh)
            S.act(junk[:], vall[:, i, :], AF.Square, accum=css[:, 0, i:i + 1])
        rstd_from_ss(css[:, 2, :], css[:, 0, :], MIXW, css[:, 1, :])
        for i in range(NT):
            S.stt("dve", vntok[:, i, :], vall[:, i, :], css[:, 2, i:i + 1], gnorm[:], ALU.mult, ALU.mult)
        wu = WS.next("Cu")
        for pr in range(2):
            for tb in range(TT):
                tok = slice(tb * 512, (tb + 1) * 512)
                bank = zbank()
                fm_matmul(bank[:, :], wu, pr * 128, lambda kc: hT[:, kc, tok])
                S.act(uT[:, pr, tok], bank[:, :], AF.Gelu_apprx_tanh)
        for i in range(NT):
            tok = slice(i * 128, (i + 1) * 128)
            mb = PS[3 + i % 2]
            for g in range(4):
                off = 64 * (g % 2)
                pr = g // 2
                S.mm(mb[off:off + 64, pr * 128:(pr + 1) * 128], vntok[:, i, g * 64:(g + 1) * 64], wsTb[:, g, :])
            tm = t2[i % 2]
            tm3 = Acc(tm.t[:].rearrange("p (a t) -> p a t", a=2), S.arena, tm[:].slots)
            mb3 = Acc(mb.t[:, 0:256].rearrange("p (a t) -> p a t", a=2), mb.space, (0,))
            S.tt("dve", tm3, mb3, bst[:], ALU.add)
            S.tt("dve", o_all[:, 2, :, tok], tm3, uT[:, :, tok], ALU.mult)

    def mixer_D(l):
        A = Alloc(S, p1, ARENA_END)
        xraw = A("xraw", [2, SEQ], F32)
        ggate = A("ggate", [2, SEQ], BF16)
        t2s = [[A("dt%d_%d" % (k, i), [512], F32) for i in range(8)] for k in range(2)]
        xcb = [A("dxcb%d" % i, [512], BF16) for i in range(2)]
        S.dma("sp", waxf[:].ap, wax_d[l], writes=[waxf[:]])
        S.copy("dve", waxb[:], waxf[:])
        wg = WS.next("Dg")
        for pr in range(2):
            for tb in range(TT):
                tok = slice(tb * 512, (tb + 1) * 512)
                bank = zbank()
                fm_matmul(bank[:, :], wg, pr * 128, lambda kc: hT[:, kc, tok])
                S.act(ggate[:, pr, tok], bank[:, :], AF.Gelu_apprx_tanh)
        wx = WS.next("Dx")
        for pr in range(2):
            for tb in range(TT):
                tok = slice(tb * 512, (tb + 1) * 512)
                bank = zbank()
                fm_matmul(bank[:, :], wx, pr * 128, lambda kc: hT[:, kc, tok])
                S.copy("act" if tb % 2 else "dve", xraw[:, pr, tok], bank[:, :])
        hh2 = [[A("dhp%d_%d" % (p_, i), [512], F32) for i in range(2)] for p_ in range(2)]
        its = [(pr, tb) for tb in range(TT) for pr in range(2)]

        def d1(k):
            pr, tb = its[k]
            t0 = tb * 512
            cw = lambda j: pcol[:, l, 36 + pr * 4 + j:37 + pr * 4 + j]
            xc, r_, ig, a_, a2, ml, bt_, _ = t2s[k % 2]
            S.ts("dve", xc[:], xraw[:, pr, t0:t0 + 512], cw(3), pcol[:, l, 34 + pr:35 + pr], ALU.mult, ALU.add)
            for j in range(3):
                sh = 3 - j
                lo = sh if tb == 0 else 0
                S.stt("dve", xc[:, lo:512], xraw[:, pr, t0 + lo - sh:t0 + 512 - sh], cw(j), xc[:, lo:512], ALU.mult, ALU.add)
            xb = xcb[k % 2]
            S.copy("act", xb[:], xc[:])
            ba_ = PS[3 + k % 2]
            bx_ = PS[5 + k % 2]
            S.mm(ba_[:, :], waxb[:, 0, pr, :], xb[:])
            S.mm(bx_[:, :], waxb[:, 1, pr, :], xb[:])
            S.act(r_[:], ba_[:, :], AF.Sigmoid, bias=pcol[:, l, 44 + pr:45 + pr])
            S.act(ig[:], bx_[:, :], AF.Sigmoid, bias=pcol[:, l, 46 + pr:47 + pr])

        def d2(k):
            pr, tb = its[k]
            t0 = tb * 512
            xc, r_, ig, a_, a2, ml, bt_, _ = t2s[k % 2]
            S.act(a_[:], r_[:], AF.Exp, scale=lruc[:, l, pr, 0:1])
            S.act(a2[:], r_[:], AF.Exp, scale=lruc[:, l, pr, 1:2])
            S.ts("pool", a2[:], a2[:], -1.0, 1.0, ALU.mult, ALU.add)
            S.act(ml[:], a2[:], AF.Sqrt)
            if tb == 0:
                S.memset("dve", ml[:, 0:1], 1.0)
            S.tt("pool", ig[:], ig[:], xc[:], ALU.mult)
            S.tt("dve", bt_[:], ml[:], ig[:], ALU.mult)
            h_ = hh2[pr][tb % 2]
            hp = hh2[pr][(tb + 1) % 2]
            if tb == 0:
                S.op("dve", lambda e: e.tensor_tensor_scan(out=h_[:].ap, data0=a_[:].ap, data1=bt_[:].ap, initial=0.0,
                                                           op0=ALU.mult, op1=ALU.add),
                     reads=[a_[:], bt_[:]], writes=[h_[:]])
            else:
                S.op("dve", lambda e: e.tensor_tensor_scan(out=h_[:].ap, data0=a_[:].ap, data1=bt_[:].ap,
                                                           initial=hp[:, 511:512].ap, op0=ALU.mult, op1=ALU.add),
                     reads=[a_[:], bt_[:], hp[:, 511:512]], writes=[h_[:]])
            S.tt("dve", o_all[:, 3, pr, t0:t0 + 512], h_[:], ggate[:, pr, t0:t0 + 512], ALU.mult)

        for k in range(len(its) + 1):
            if k < len(its):
                d1(k)
            if k >= 1:
                d2(k - 1)

    def block_phase(l, tb, last_layer):
        A = Alloc(S, p1, ARENA_END)
        tok = slice(tb * 512, (tb + 1) * 512)
        xt = A("xt", [NKC, 512], F32)
        yT = A("yT", [NKC, 512], F32)
        macc = A("macc", [NKC, 512], F32)
        mrg = A("mrg", [NKC, 512], BF16)
        hid = View(S, "hid", [NJ, 512], BF16, macc.off)
        sg = [A("sg%d" % i, [512], F32) for i in range(2)]
        tp = [A("tp%d" % i, [512], F32) for i in range(2)]
        A2 = Alloc(S, A.cur, ARENA_END)
        S.dma("sp", xt[:].ap, xT_d[:, :, tok], reads=[Acc(None, xT_space, (tb,))], writes=[xt[:]])
        k = 0
        for n in range(4):
            for q4 in range(4):
                wgt = WS.next("G")
                wb = WS.next("WB")
                for d2 in range(2):
                    dc = q4 * 2 + d2
                    gb = zbank()
                    fm_matmul(gb[:, :], wgt, d2 * 128, lambda kc: hT[:, kc, tok])
                    pb = zbank()
                    for kc in range(2):
                        S.mm(pb[:, :], wb[:, kc, d2 * 128:(d2 + 1) * 128], o_all[:, n, kc, tok], start=(kc == 0), stop=(kc == 1))
                    s_ = sg[k % 2]
                    S.act(s_[:], gb[:, :], AF.Sigmoid)
                    if n == 0:
                        S.tt("dve", macc[:, dc, :], s_[:], pb[:, :], ALU.mult)
                    else:
                        t_ = tp[k % 2]
                        S.tt("dve", t_[:], s_[:], pb[:, :], ALU.mult)
                        if n < 3:
                            S.tt("pool", macc[:, dc, :], macc[:, dc, :], t_[:], ALU.add)
                        else:
                            S.tt("pool", mrg[:, dc, :], macc[:, dc, :], t_[:], ALU.add)
                    k += 1
        ev = Evac(yT, 8, l, A2)
        for q4 in range(4):
            wo = WS.next("WO")
            for d2 in range(2):
                dc = q4 * 2 + d2
                yb = zbank()
                fm_matmul(yb[:, :], wo, d2 * 128, lambda kc: mrg[:, kc, :])
                ev.chunk(dc, yb)
        ev.finish()
        resid_norm(xt, yT, l, 16, tb, A2, ev.sq)
        for j2 in range(NJ // 2):
            w1g = WS.next("F1g")
            w1u = WS.next("F1u")
            for jj in range(2):
                j = j2 * 2 + jj
                gb = zbank()
                fm_matmul(gb[:, :], w1g, jj * 128, lambda kc: hT[:, kc, tok])
                ub = zbank()
                fm_matmul(ub[:, :], w1u, jj * 128, lambda kc: hT[:, kc, tok])
                s_ = sg[j % 2]
                S.act(s_[:], gb[:, :], AF.Silu)
                S.tt("dve", hid[:, j, :], s_[:], ub[:, :], ALU.mult)
        A2.cur = A.cur
        ev = Evac(yT, 24, l, A2)
        for dc in range(NKC):
            wa_ = WS.next("F2")
            wb_ = WS.next("F2")
            yb = zbank()
            fm_matmul(yb[:, :], wa_, 0, lambda kc: hid[:, kc, :], nk=11, k0=0, first=True, last=False)
            fm_matmul(yb[:, :], wb_, 0, lambda kc: hid[:, kc, :], nk=11, k0=11, first=False, last=True)
            ev.chunk(dc, yb)
        ev.finish()
        resid_norm(xt, yT, None if last_layer else l + 1, 0, tb, A2, ev.sq)
        if not last_layer:
            S.dma("sp", xT_d[:, :, tok], xt[:].ap, reads=[xt[:]], writes=[Acc(None, xT_space, (tb,))])
        else:
            A2.cur = A.cur
            otk = [A2("otk%d" % i, [D], F32) for i in range(2)]
            for i4 in range(4):
                ok = otk[i4 % 2]
                for half in range(2):
                    bank = PS[3 + (2 * i4 + half) % 2]
                    for c4 in range(4):
                        c = half * 4 + c4
                        S.transpose(bank[:, c4 * 128:(c4 + 1) * 128], xt[:, c, i4 * 128:(i4 + 1) * 128], identf[:])
                    S.copy("act" if half else "dve", ok[:, half * 512:(half + 1) * 512], bank[:, :])
                r0 = tb * 512 + i4 * 128
                S.dma("sp", out_d[r0:r0 + 128, :], ok[:].ap, reads=[ok[:]], is_output=True)

    def emit():
        if stop == "const":
            return
        load_input()
        if stop == "load":
            return
        for l in range(DEPTH):
            for nm, fn in (("A", mixer_A), ("B", mixer_B), ("C", mixer_C), ("D", mixer_D)):
                fn(l)
                if stop is not None and stop.startswith(nm):
                    return
            if dbg and l == 0:
                S.dma("sp", dbg_d, o_all[:].ap, reads=[o_all[:]], is_output=True)
            for tb in range(TT):
                block_phase(l, tb, l == DEPTH - 1)
    emit()
    if stop is not None and dbg:
        S.dma("sp", dbg_d, o_all[:].ap, reads=[o_all[:]], is_output=True)
    S.finish()
    return nc, S


def prep_shared(inp, DEPTH):
    f = lambda k: np.ascontiguousarray(np.asarray(inp[k], dtype=np.float32))
    pcol = np.zeros((DEPTH, 128, NPCOL), np.float32)

    def fm8(v):
        return v.reshape(DEPTH, 8, 128).transpose(0, 2, 1)

    def fm2(v):
        return v.reshape(DEPTH, 2, 128).transpose(0, 2, 1)

    pcol[:, :, 0:8] = fm8(f("norm_mix_pre"))
    pcol[:, :, 8:16] = fm8(f("norm_mix_post"))
    pcol[:, :, 16:24] = fm8(f("norm_ffn_pre"))
    pcol[:, :, 24:32] = fm8(f("norm_ffn_post"))
    pcol[:, :, 32:34] = fm2(f("hgrn_norm_g"))
    pcol[:, :, 34:36] = fm2(f("lru_conv_b"))
    cw = f("lru_conv_w")
    for pr in range(2):
        for j in range(4):
            pcol[:, :, 36 + pr * 4 + j] = cw[:, j, pr * 128:(pr + 1) * 128]
    pcol[:, :, 44:46] = fm2(f("lru_ba"))
    pcol[:, :, 46:48] = fm2(f("lru_bx"))
    pcol[:, :, 48:50] = fm2(f("lru_lambda"))
    lg = f("hgrn_lb_logits")
    pcol[:, :, 50:58] = NEG
    for pr in range(2):
        for d in range(DEPTH):
            pcol[:, :, 50 + pr * 4 + d] = lg[d, pr * 128:(pr + 1) * 128][None, :]
    gnorm_bc = np.ascontiguousarray(np.broadcast_to(f("gmlp_norm_g")[:, None, :], (DEPTH, 128, MIXW)))
    wsT = np.ascontiguousarray(f("gmlp_ws").transpose(0, 3, 1, 2))
    bs = f("gmlp_bs")
    bs_t = np.zeros((DEPTH, 128, 2, 128), np.float32)
    for g in range(4):
        bs_t[:, (g % 2) * 64:(g % 2) * 64 + 64, g // 2, :] = bs[:, g, None, :]
    wax = np.zeros((DEPTH, 128, 2, 2, 128), np.float32)
    for ai, key in enumerate(("lru_wa", "lru_wx")):
        w = f(key)
        for h in range(4):
            o = (h % 2) * 64
            wax[:, o:o + 64, ai, h // 2, o:o + 64] = w[:, h]
    rb = f("attn_rel_bias")
    k = np.arange(128)[:, None]
    q = np.arange(128)[None, :]
    abias = np.zeros((DEPTH, 128, 5, 4, 128), np.float32)
    for j in range(5):
        dist = (4 - j) * 128 + q - k
        idx = np.clip(dist, -63, 256) + 63
        cq = 8 + q // 64
        ck = 2 * j + k // 64
        valid = (ck >= cq - 8) & (ck <= cq)
        g = rb[:, :, idx]
        g = np.where(valid[None, None], g, np.float32(NEG))
        abias[:, :, j] = g.transpose(0, 2, 1, 3)
    cm = np.zeros((128, 4, 512), np.float32)
    s = np.arange(128)[:, None]
    t = np.arange(128)[None, :]
    m0 = ((s // 64) == (t // 64)) & (s <= t)
    cm[:, 0, :] = np.tile(m0.astype(np.float32), (1, 4))
    cm[:, 1, :] = (np.arange(512) % 64 != 0).astype(np.float32)[None, :]
    cm[:, 2, 0:128] = (s <= t).astype(np.float32)
    cm[:, 3, 0:128] = np.eye(128, dtype=np.float32)
    return {
        "w_in": f("w_in"), "w_branch": f("w_branch"), "w_out": f("w_out"),
        "w_ffn_in": f("w_ffn_in"), "w_ffn_out": f("w_ffn_out"),
        "pcol": pcol, "gnorm_bc": gnorm_bc, "wsT": wsT, "bs_t": bs_t, "wax": wax,
        "abias": abias, "cmasks": cm,
    }


_CACHE = {}


def kernel(**inputs):
    x = np.asarray(inputs["x"], dtype=np.float32)
    B, SEQ, _ = x.shape
    DEPTH = inputs["w_in"].shape[0]
    key = (SEQ, DEPTH)
    if key not in _CACHE:
        _CACHE[key] = build_program(SEQ, DEPTH)[0]
    nc = _CACHE[key]
    shared = prep_shared(inputs, DEPTH)
    in_maps = []
    for b in range(B):
        m = dict(shared)
        m["x"] = np.ascontiguousarray(x[b])
        in_maps.append(m)
    res = run_bass_kernel_spmd(nc, in_maps, core_ids=list(range(B)))
    return np.stack([np.asarray(r["out"], dtype=np.float32) for r in res.results], axis=0)
```

```python
import itertools
import numpy as np
import concourse.bass as bass
import concourse.mybir as mybir
from concourse.bass_utils import run_bass_kernel_spmd

F32 = mybir.dt.float32
BF16 = mybir.dt.bfloat16
AF = mybir.ActivationFunctionType
ALU = mybir.AluOpType
AX = mybir.AxisListType

D = 1024
NKC = 8
MIXW = 256
IN_COLS = 6912
FFH = 2816
NJ = 22
EPS = 1e-6
NEG = -30000.0
SLOT = 1024
NPCOL = 64


class Space:
    def __init__(self, n):
        self.lastw = [None] * n
        self.reads = [dict() for _ in range(n)]


class Acc:
    __slots__ = ("ap", "space", "slots")

    def __init__(self, ap, space, slots):
        self.ap = ap
        self.space = space
        self.slots = slots


class View:
    def __init__(self, S, name, fshape, dtype, off):
        self.S = S
        self.fshape = list(fshape)
        self.esz = mybir.dt.size(dtype)
        self.off = off
        self.nbytes = int(np.prod(fshape)) * self.esz
        assert off % 32 == 0 and off + self.nbytes <= S.arena_bytes, (name, off, self.nbytes)
        S.nview += 1
        self.t = S.nc.alloc_sbuf_tensor_at("%s_%d" % (name, S.nview), [128] + self.fshape, dtype,
                                           offset=S.arena_base + off)
        st = [1] * len(fshape)
        for i in range(len(fshape) - 2, -1, -1):
            st[i] = st[i + 1] * fshape[i + 1]
        self.strides = st

    def __getitem__(self, key):
        if not isinstance(key, tuple):
            key = (key,)
        fk = list(key[1:]) + [slice(None)] * (len(self.fshape) - (len(key) - 1))
        rng = []
        for k, n in zip(fk, self.fshape):
            if isinstance(k, int):
                rng.append((k, k + 1))
            else:
                lo = 0 if k.start is None else k.start
                hi = n if k.stop is None else k.stop
                rng.append((lo, hi))
        nd = len(rng)
        last = nd - 1
        while last > 0 and rng[last] == (0, self.fshape[last]):
            last -= 1
        slots = set()
        inner = self.strides[last]
        for idx in itertools.product(*[range(a, b) for a, b in rng[:last]]):
            base = sum(i * s for i, s in zip(idx, self.strides))
            b0 = self.off + (base + rng[last][0] * inner) * self.esz
            b1 = self.off + (base + rng[last][1] * inner) * self.esz
            slots.update(range(b0 // SLOT, (b1 - 1) // SLOT + 1))
        return Acc(self.t[key], self.S.arena, sorted(slots))


class PBank:
    def __init__(self, S, i):
        self.t = S.nc.alloc_psum_tensor("psb%d" % i, [128, 512], F32)
        self.space = Space(1)

    def __getitem__(self, key):
        return Acc(self.t[key], self.space, (0,))

    def bf(self, key):
        return Acc(self.t[:].bitcast(BF16)[key], self.space, (0,))


class Sched:
    ENG = ("pe", "dve", "act", "pool", "sp")

    def __init__(self, nc, arena_kib=206, n_dma_sems=6):
        self.nc = nc
        self.e = {"pe": nc.tensor, "dve": nc.vector, "act": nc.scalar,
                  "pool": nc.gpsimd, "sp": nc.sync}
        self.sem = {k: nc.alloc_semaphore("s_" + k) for k in self.ENG}
        self.cnt = {k: 0 for k in self.ENG}
        self.dsem, self.dcnt, self.drr = {}, {}, {}
        for q in ("sp", "pool"):
            self.dsem[q] = [nc.alloc_semaphore("d_%s%d" % (q, i)) for i in range(n_dma_sems)]
            self.dcnt[q] = [0] * n_dma_sems
            self.drr[q] = 0
        self.waited = {k: {} for k in self.ENG}
        self.out_waits = []
        self.nview = 0
        self.arena_bytes = arena_kib * 1024
        nc.alloc_sbuf_tensor("arena", [128, self.arena_bytes], mybir.dt.uint8)
        self.arena_base = list(nc.allocations)[-1].memorylocations[0].addr
        self.arena = Space(self.arena_bytes // SLOT)
        self.banks = [PBank(self, i) for i in range(8)]
        self.nops = 0

    def _semof(self, key):
        if isinstance(key, str):
            return self.sem[key]
        return self.dsem[key[0]][key[1]]

    def _wait(self, eng, key, val):
        w = self.waited[eng]
        if w.get(key, 0) >= val:
            return
        self.e[eng].wait_ge(self._semof(key), val)
        w[key] = val

    def _deps(self, eng, reads, writes):
        deps = {}
        for a in reads:
            sp = a.space
            for s in a.slots:
                d = sp.lastw[s]
                if d is not None and deps.get(d[0], 0) < d[1]:
                    deps[d[0]] = d[1]
        for a in writes:
            sp = a.space
            for s in a.slots:
                d = sp.lastw[s]
                if d is not None and deps.get(d[0], 0) < d[1]:
                    deps[d[0]] = d[1]
                for k, v in sp.reads[s].items():
                    if deps.get(k, 0) < v:
                        deps[k] = v
        for k, v in deps.items():
            if k == eng and eng == "pe":
                continue
            self._wait(eng, k, v)

    def _record(self, key, val, reads, writes):
        for a in reads:
            sp = a.space
            for s in a.slots:
                sp.reads[s][key] = val
        for a in writes:
            sp = a.space
            for s in a.slots:
                sp.lastw[s] = (key, val)
                sp.reads[s] = {}

    def op(self, eng, fn, reads=(), writes=()):
        self._deps(eng, reads, writes)
        ins = fn(self.e[eng])
        self.cnt[eng] += 1
        ins.then_inc(self.sem[eng], 1)
        self._record(eng, self.cnt[eng], reads, writes)
        self.nops += 1
        return ins

    def dma(self, q, out, in_, reads=(), writes=(), is_output=False):
        i = self.drr[q]
        self.drr[q] = (i + 1) % len(self.dsem[q])
        key = (q, i)
        if self.dcnt[q][i] > 0:
            self._wait(q, key, self.dcnt[q][i])
        self._deps(q, reads, writes)
        ins = self.e[q].dma_start(out=out, in_=in_)
        self.dcnt[q][i] += 16
        ins.then_inc(self.dsem[q][i], 16)
        self._record(key, self.dcnt[q][i], reads, writes)
        if is_output:
            self.out_waits.append((key, self.dcnt[q][i]))
        self.nops += 1
        return ins

    def finish(self):
        for key, v in self.out_waits:
            self._wait("sp", key, v)
        for k in ("pe", "dve", "act", "pool"):
            if self.cnt[k]:
                self._wait("sp", k, self.cnt[k])

    def mm(self, out, lhsT, rhs, start=True, stop=True, skip=False):
        return self.op("pe", lambda e: e.matmul(out.ap, lhsT=lhsT.ap, rhs=rhs.ap, start=start, stop=stop,
                                                skip_group_check=skip),
                       reads=[lhsT, rhs], writes=[out])

    def transpose(self, out, in_, ident):
        return self.op("pe", lambda e: e.transpose(out.ap, in_.ap, ident.ap), reads=[in_, ident], writes=[out])

    def act(self, out, in_, func, scale=None, bias=None, accum=None, eng="act"):
        kw = {}
        rd = [in_]
        wr = [out]
        if scale is not None:
            if isinstance(scale, Acc):
                rd.append(scale)
                kw["scale"] = scale.ap
            else:
                kw["scale"] = scale
        if bias is not None:
            if isinstance(bias, Acc):
                rd.append(bias)
                kw["bias"] = bias.ap
            else:
                kw["bias"] = bias
        if accum is not None:
            wr.append(accum)
            kw["accum_out"] = accum.ap
        return self.op(eng, lambda e: e.activation(out=out.ap, in_=in_.ap, func=func, **kw), reads=rd, writes=wr)

    def tt(self, eng, out, in0, in1, op):
        return self.op(eng, lambda e: e.tensor_tensor(out=out.ap, in0=in0.ap, in1=in1.ap, op=op),
                       reads=[in0, in1], writes=[out])

    def ts(self, eng, out, in0, s1, s2, op0, op1=None):
        rd = [in0]
        a1 = s1.ap if isinstance(s1, Acc) else s1
        a2 = s2.ap if isinstance(s2, Acc) else s2
        if isinstance(s1, Acc):
            rd.append(s1)
        if isinstance(s2, Acc):
            rd.append(s2)
        if op1 is None:
            return self.op(eng, lambda e: e.tensor_scalar(out=out.ap, in0=in0.ap, scalar1=a1, scalar2=None, op0=op0),
                           reads=rd, writes=[out])
        return self.op(eng, lambda e: e.tensor_scalar(out=out.ap, in0=in0.ap, scalar1=a1, scalar2=a2, op0=op0, op1=op1),
                       reads=rd, writes=[out])

    def stt(self, eng, out, in0, scalar, in1, op0, op1):
        rd = [in0, in1]
        sc = scalar.ap if isinstance(scalar, Acc) else scalar
        if isinstance(scalar, Acc):
            rd.append(scalar)
        return self.op(eng, lambda e: e.scalar_tensor_tensor(out=out.ap, in0=in0.ap, scalar=sc, in1=in1.ap, op0=op0, op1=op1),
                       reads=rd, writes=[out])

    def copy(self, eng, out, in_):
        if eng == "act":
            return self.act(out, in_, AF.Copy)
        return self.op(eng, lambda e: e.tensor_copy(out=out.ap, in_=in_.ap), reads=[in_], writes=[out])

    def memset(self, eng, out, val):
        return self.op(eng, lambda e: e.memset(out.ap, val), writes=[out])


def bcast(acc, shape):
    return Acc(acc.ap.unsqueeze(len(acc.ap.shape)).to_broadcast(list(shape)), acc.space, acc.slots)


class Alloc:
    def __init__(self, S, lo, hi):
        self.S, self.lo, self.hi, self.cur = S, lo, hi, lo

    def __call__(self, name, fshape, dtype, own=False):
        n = int(np.prod(fshape)) * mybir.dt.size(dtype)
        al = SLOT if (n >= SLOT or own) else 64
        off = (self.cur + al - 1) // al * al
        assert off + n <= self.hi, ("arena region overflow", name, off, n, self.hi)
        self.cur = off + n
        if n >= SLOT or own:
            self.cur = (self.cur + SLOT - 1) // SLOT * SLOT
        return View(self.S, name, fshape, dtype, off)


class WStream:
    NSLOT = 8
    SLOT_EL = 2048
    AHEAD = 5

    def __init__(self, S, alloc, plan):
        self.S = S
        self.plan = plan
        self.slots = [alloc("wslot%d" % i, [self.SLOT_EL], BF16) for i in range(self.NSLOT)]
        self.views = {}
        self.issued = 0
        self.k = 0

    def _view(self, si, a, b):
        key = (si, a, b)
        if key not in self.views:
            sl = self.slots[si]
            self.views[key] = View(self.S, "wv", [a, b], BF16, sl.off)
        return self.views[key]

    def _issue(self):
        tag, src, (a, b) = self.plan[self.issued]
        v = self._view(self.issued % self.NSLOT, a, b)
        self.S.dma("pool", v[:].ap, src, writes=[v[:]])
        self.issued += 1

    def next(self, tag):
        k = self.k
        assert self.plan[k][0] == tag, (k, self.plan[k][0], tag)
        while self.issued < min(len(self.plan), k + self.AHEAD + 1):
            self._issue()
        self.k += 1
        _, _, (a, b) = self.plan[k]
        return self._view(k % self.NSLOT, a, b)


def build_program(SEQ, DEPTH, dbg=False, stop=None):
    NT = SEQ // 128
    TT = SEQ // 512
    NCH = SEQ // 64
    nc = bass.Bass("TRN2", target_bir_lowering=False)
    S = Sched(nc)
    PS = S.banks

    def dram(name, shape, dt=F32, kind="ExternalInput"):
        return nc.dram_tensor(name, list(shape), dt, kind=kind).ap()

    x_in = dram("x", [SEQ, D])
    w_in = dram("w_in", [DEPTH, D, IN_COLS])
    w_br = dram("w_branch", [DEPTH, 4, MIXW, D])
    w_out = dram("w_out", [DEPTH, D, D])
    w_f1 = dram("w_ffn_in", [DEPTH, D, 2 * FFH])
    w_f2 = dram("w_ffn_out", [DEPTH, FFH, D])
    pcol_d = dram("pcol", [DEPTH, 128, NPCOL])
    gnorm_d = dram("gnorm_bc", [DEPTH, 128, MIXW])
    wsT_d = dram("wsT", [DEPTH, 128, 4, 128])
    bst_d = dram("bs_t", [DEPTH, 128, 2, 128])
    wax_d = dram("wax", [DEPTH, 128, 2, 2, 128])
    abias_d = dram("abias", [DEPTH, 128, 5, 4, 128])
    cmask_d = dram("cmasks", [128, 4, 512])
    out_d = dram("out", [SEQ, D], kind="ExternalOutput")
    xT_d = dram("xT_scratch", [128, NKC, SEQ], kind="Internal")
    xT_space = Space(TT)
    if dbg:
        dbg_d = dram("dbg_oall", [128, 4, 2, SEQ], BF16, kind="ExternalOutput")

    KB = 1024
    A_const = Alloc(S, 0, 24 * KB)
    A_w = Alloc(S, 24 * KB, 56 * KB)
    A_h = Alloc(S, 56 * KB, 56 * KB + 16 * SEQ)
    p0 = 56 * KB + 16 * SEQ
    A_o = Alloc(S, p0, p0 + 16 * SEQ)
    p1 = p0 + 16 * SEQ
    ARENA_END = S.arena_bytes

    hT = A_h("hT", [NKC, SEQ], BF16)
    o_all = A_o("o_all", [4, 2, SEQ], BF16)

    ident = A_const("ident", [128], BF16)
    ones_bf = A_const("ones", [128], BF16)
    bones = A_const("bones", [128], BF16)
    cm = A_const("cmasks", [4, 512], F32)
    pcol = A_const("pcol", [DEPTH, NPCOL], F32)
    lbt = A_const("lbt", [DEPTH, 2, 2], F32)
    lruc = A_const("lruc", [DEPTH, 2, 2], F32)
    tmpc = A_const("tmpc", [32], F32)
    gnorm = A_const("gnorm", [MIXW], F32)
    wsTf = A_const("wsTf", [4, 128], F32)
    wsTb = A_const("wsTb", [4, 128], BF16)
    bst = A_const("bst", [2, 128], F32)
    waxf = A_const("waxf", [2, 2, 128], F32)
    waxb = A_const("waxb", [2, 2, 128], BF16)
    decs = A_const("decs", [2, NCH], F32)
    rcol = A_const("rcol", [16], F32)

    def wsrc(w2d, c0, ncols, k0=0, nk=NKC):
        return w2d[k0 * 128:(k0 + nk) * 128, c0:c0 + ncols].rearrange("(kc p) c -> p kc c", p=128)

    plan = []
    for l in range(DEPTH):
        wi = w_in[l]
        plan.append(("Aq", wsrc(wi, 0, 256), (NKC, 256)))
        plan.append(("Ak", wsrc(wi, 256, 256), (NKC, 256)))
        plan.append(("Av", wsrc(wi, 512, 256), (NKC, 256)))
        plan.append(("Bf", wsrc(wi, 1024, 256), (NKC, 256)))
        plan.append(("Bq", wsrc(wi, 768, 256), (NKC, 256)))
        plan.append(("Bi", wsrc(wi, 1280, 256), (NKC, 256)))
        plan.append(("Bg", wsrc(wi, 1536, 256), (NKC, 256)))
        plan.append(("Cv", wsrc(wi, 2048, 256), (NKC, 256)))
        plan.append(("Cu", wsrc(wi, 1792, 256), (NKC, 256)))
        plan.append(("Dg", wsrc(wi, 2560, 256), (NKC, 256)))
        plan.append(("Dx", wsrc(wi, 2304, 256), (NKC, 256)))
        for tb in range(TT):
            for n in range(4):
                for q4 in range(4):
                    plan.append(("G", wsrc(wi, 2816 + n * 1024 + q4 * 256, 256), (NKC, 256)))
                    plan.append(("WB", wsrc(w_br[l, n], q4 * 256, 256, 0, 2), (2, 256)))
            for q4 in range(4):
                plan.append(("WO", wsrc(w_out[l], q4 * 256, 256), (NKC, 256)))
            for j2 in range(NJ // 2):
                plan.append(("F1g", wsrc(w_f1[l], j2 * 256, 256), (NKC, 256)))
                plan.append(("F1u", wsrc(w_f1[l], FFH + j2 * 256, 256), (NKC, 256)))
            for dc in range(NKC):
                for kh in range(2):
                    plan.append(("F2", wsrc(w_f2[l], dc * 128, 128, kh * 11, 11), (11, 128)))
    WS = WStream(S, A_w, plan)

    S.dma("sp", cm[:].ap, cmask_d, writes=[cm[:]])
    S.dma("sp", pcol[:].ap, pcol_d.rearrange("l p c -> p l c"), writes=[pcol[:]])
    S.memset("dve", ones_bf[:], 1.0)
    S.memset("dve", bones[:], 0.0)
    S.memset("dve", bones[0:64, 0:64], 1.0)
    S.memset("dve", bones[64:128, 64:128], 1.0)
    class _IdF:
        def __getitem__(self, key):
            return cm[:, 3, 0:128]
    identf = _IdF()
    S.copy("dve", ident[:], identf[:])
    lg = pcol[:, 0, 50:58]
    S.act(tmpc[:, 0:8], lg, AF.Exp)
    e3 = Acc(tmpc[:, 0:8].ap.rearrange("p (a d) -> p a d", a=2), S.arena, tmpc[:, 0:8].slots)
    S.op("dve", lambda e: e.reduce_sum(out=tmpc[:, 8:10].ap, in_=e3.ap, axis=AX.X), reads=[e3], writes=[tmpc[:, 8:10]])
    S.op("dve", lambda e: e.reciprocal(out=tmpc[:, 10:12].ap, in_=tmpc[:, 8:10].ap), reads=[tmpc[:, 8:10]], writes=[tmpc[:, 10:12]])
    for pr in range(2):
        S.ts("dve", tmpc[:, 12 + pr * 4:16 + pr * 4], tmpc[:, pr * 4:pr * 4 + 4], tmpc[:, 10 + pr:11 + pr], None, ALU.mult)
    for l in range(DEPTH):
        for pr in range(2):
            if l == 0:
                S.memset("dve", lbt[:, l, pr, 0:1], 0.0)
            else:
                S.tt("dve", lbt[:, l, pr, 0:1], lbt[:, l - 1, pr, 0:1], tmpc[:, 12 + pr * 4 + l:13 + pr * 4 + l], ALU.add)
            S.ts("dve", lbt[:, l, pr, 1:2], lbt[:, l, pr, 0:1], -1.0, 1.0, ALU.mult, ALU.add)
    for l in range(DEPTH):
        S.act(tmpc[:, 20:22], pcol[:, l, 48:50], AF.Exp, scale=-1.0)
        S.act(tmpc[:, 22:24], tmpc[:, 20:22], AF.Ln, bias=1.0)
        for pr in range(2):
            S.ts("dve", lruc[:, l, pr, 0:1], tmpc[:, 22 + pr:23 + pr], -8.0, None, ALU.mult)
            S.ts("dve", lruc[:, l, pr, 1:2], tmpc[:, 22 + pr:23 + pr], -16.0, None, ALU.mult)

    zring = [0]

    def zbank():
        b = PS[zring[0] % 3]
        zring[0] += 1
        return b

    def fm_matmul(bank, wv, c0, rhs_fn, nk=NKC, k0=0, first=True, last=True, ncols=128):
        for kc in range(nk):
            S.mm(bank, wv[:, kc, c0:c0 + ncols], rhs_fn(k0 + kc),
                 start=(first and kc == 0), stop=(last and kc == nk - 1))

    def rstd_from_ss(out, ss_bank, n, tmp):
        S.act(tmp, ss_bank, AF.Ln, scale=1.0 / n, bias=EPS)
        S.act(out, tmp, AF.Exp, scale=-0.5)

    def norm_to_hT(xt, gcol0, l, tb, A):
        sq = A("nsq", [NKC, 512], BF16)
        lnv = A("nln", [512], F32)
        rs = A("nrs", [512], F32)
        bank = PS[7]
        for c in range(NKC):
            S.tt("pool", sq[:, c, :], xt[:, c, :], xt[:, c, :], ALU.mult)
        for c in range(NKC):
            S.mm(bank[:, :], ones_bf[:], sq[:, c, :], start=(c == 0), stop=(c == NKC - 1))
        rstd_from_ss(rs[:], bank[:, :], D, lnv[:])
        for c in range(NKC):
            S.stt("dve", hT[:, c, tb * 512:(tb + 1) * 512], xt[:, c, :], pcol[:, l, gcol0 + c:gcol0 + c + 1], rs[:],
                  ALU.mult, ALU.mult)

    class Evac:
        def __init__(self, yT, gcol0, l, A):
            self.yT, self.g0, self.l = yT, gcol0, l
            self.sq = A("esq", [NKC, 512], BF16)
            self.pending = None

        def _ss(self, c):
            S.mm(PS[7][:, :], ones_bf[:], self.sq[:, c, :], start=(c == 0), stop=(c == NKC - 1))

        def chunk(self, dc, bank):
            S.act(self.yT[:, dc, :], bank[:, :], AF.Copy, scale=pcol[:, self.l, self.g0 + dc:self.g0 + dc + 1])
            S.act(self.sq[:, dc, :], bank[:, :], AF.Square)
            if self.pending is not None:
                self._ss(self.pending)
            self.pending = dc

        def finish(self):
            self._ss(self.pending)

    def resid_norm(xt, yT, l_next, gcol_next, tb, A, sq):
        lnv = A("rln", [512], F32)
        rs = A("rrs", [512], F32)
        rstd_from_ss(rs[:], PS[7][:, :], D, lnv[:])
        for c in range(NKC):
            S.tt("dve", yT[:, c, :], yT[:, c, :], rs[:], ALU.mult)
            S.tt("dve", xt[:, c, :], xt[:, c, :], yT[:, c, :], ALU.add)
            if l_next is not None:
                S.act(sq[:, c, :], xt[:, c, :], AF.Square)
        if l_next is None:
            return
        bank = PS[6]
        for c in range(NKC):
            S.mm(bank[:, :], ones_bf[:], sq[:, c, :], start=(c == 0), stop=(c == NKC - 1))
        lnv2 = A("rln2", [512], F32)
        rs2 = A("rrs2", [512], F32)
        rstd_from_ss(rs2[:], bank[:, :], D, lnv2[:])
        for c in range(NKC):
            S.stt("dve", hT[:, c, tb * 512:(tb + 1) * 512], xt[:, c, :], pcol[:, l_next, gcol_next + c:gcol_next + c + 1], rs2[:],
                  ALU.mult, ALU.mult)

    def load_input():
        A = Alloc(S, p1, ARENA_END)
        xtok = [A("xtok%d" % i, [D], F32) for i in range(2)]
        xts = [A("xt%d" % i, [NKC, 512], F32) for i in range(2)]
        for tb in range(TT):
            xt = xts[tb % 2]
            for i4 in range(4):
                i = tb * 4 + i4
                xk = xtok[i % 2]
                S.dma("sp", xk[:].ap, x_in[i * 128:(i + 1) * 128, :], writes=[xk[:]])
                for half in range(2):
                    bank = PS[3 + (2 * i + half) % 2]
                    for c4 in range(4):
                        c = half * 4 + c4
                        S.transpose(bank[:, c4 * 128:(c4 + 1) * 128], xk[:, c * 128:(c + 1) * 128], identf[:])
                    src = Acc(bank.t[:].rearrange("p (c t) -> p c t", c=4), bank.space, (0,))
                    S.copy("act" if half else "dve", xt[:, half * 4:(half + 1) * 4, i4 * 128:(i4 + 1) * 128], src)
            S.dma("sp", xT_d[:, :, tb * 512:(tb + 1) * 512], xt[:].ap, reads=[xt[:]],
                  writes=[Acc(None, xT_space, (tb,))])
            A2 = Alloc(S, A.cur, ARENA_END)
            norm_to_hT(xt, 0, 0, tb, A2)

    def mixer_A(l):
        A = Alloc(S, p1, ARENA_END)
        qT = A("qTm", [4, SEQ], BF16)
        kT = A("kT", [2, SEQ], BF16)
        vaug = A("vaug", [NT, 4, 65], BF16)
        ab = A("abias", [5, 4, 128], F32)
        SK = 3
        tmp = [A("atmp%d" % i, [512], F32) for i in range(SK + 1)]
        PT = [A("aPT%d" % i, [512], BF16) for i in range(SK + 1)]
        otok = [A("aotok%d" % i, [256], BF16, own=True) for i in range(2)]
        rec = [A("arec%d" % i, [4], F32, own=True) for i in range(2)]
        S.dma("sp", ab[:].ap, abias_d[l], writes=[ab[:]])
        S.memset("dve", vaug[:], 1.0)
        S.memset("pool", qT[:], 0.0)
        for c in range(4):
            if c % 2 == 0:
                wqk = WS.next("Aq" if c == 0 else "Ak")
            for tb in range(TT):
                bank = zbank()
                fm_matmul(bank[:, :], wqk, (c % 2) * 128, lambda kc: hT[:, kc, tb * 512:(tb + 1) * 512])
                if c < 2:
                    S.act(qT[0:64, 2 * c, tb * 512:(tb + 1) * 512], bank[0:64, :], AF.Copy, scale=0.125)
                    S.act(qT[64:128, 2 * c + 1, tb * 512:(tb + 1) * 512], bank[64:128, :], AF.Copy, scale=0.125)
                else:
                    S.copy("dve", kT[:, c - 2, tb * 512:(tb + 1) * 512], bank[:, :])
        if stop == "A1":
            return
        wv = WS.next("Av")
        for i in range(NT):
            bank = zbank()
            for kc in range(NKC):
                S.mm(bank[:, 0:256], hT[:, kc, i * 128:(i + 1) * 128], wv[:, kc, :], start=(kc == 0), stop=(kc == NKC - 1))
            src = Acc(bank.t[:, 0:256].rearrange("p (h d) -> p h d", h=4), bank.space, (0,))
            S.copy("act" if i % 2 else "dve", vaug[:, i, :, 0:64], src)
        if stop == "A2":
            return
        steps = [(m, kt) for m in range(NT) for kt in range(max(0, m - 4), m + 1)]

        def qk(si):
            m, kt = steps[si]
            bank = PS[si % (SK + 1)]
            for h in range(4):
                S.mm(bank[:, h * 128:(h + 1) * 128], kT[:, h // 2, kt * 128:(kt + 1) * 128],
                     qT[:, h, m * 128:(m + 1) * 128])
            j = kt - m + 4
            bsrc = Acc(ab.t[:, j].rearrange("p h q -> p (h q)"), S.arena, ab[:, j].slots)
            S.tt("dve", tmp[si % (SK + 1)][:], bank[:, :], bsrc, ALU.add)
            S.act(PT[si % (SK + 1)][:], tmp[si % (SK + 1)][:], AF.Exp)

        def pv(si):
            m, kt = steps[si]
            first = kt == max(0, m - 4)
            last = kt == m
            ob = PS[4 + m % 2]
            for h in range(4):
                S.mm(ob[:, h * 65:(h + 1) * 65], PT[si % (SK + 1)][:, h * 128:(h + 1) * 128], vaug[:, kt, h, :],
                     start=(first and h == 0), stop=last, skip=True)
            if last:
                o3 = Acc(ob.t[:, 0:260].rearrange("p (h d) -> p h d", h=4), ob.space, (0,))
                rc = rec[m % 2]
                S.op("dve", lambda e: e.reciprocal(out=rc[:].ap, in_=o3.ap[:, :, 64]), reads=[o3], writes=[rc[:]])
                ot = otok[m % 2]
                o3v = Acc(o3.ap[:, :, 0:64], ob.space, (0,))
                otv = Acc(ot.t[:].rearrange("p (h d) -> p h d", h=4), S.arena, ot[:].slots)
                S.tt("dve", otv, o3v, bcast(rc[:], [128, 4, 64]), ALU.mult)
                tb_ = PS[6 + m % 2]
                for pr in range(2):
                    S.transpose(tb_.bf((slice(None), slice(pr * 128, (pr + 1) * 128))), ot[:, pr * 128:(pr + 1) * 128], ident[:])
                srcT = Acc(tb_.t[:].bitcast(BF16)[:, 0:256].rearrange("p (a t) -> p a t", a=2), tb_.space, (0,))
                S.copy("act", o_all[:, 0, :, m * 128:(m + 1) * 128], srcT)

        for si in range(len(steps) + SK):
            if si < len(steps):
                qk(si)
            if si >= SK:
                pv(si - SK)

    def mixer_B(l):
        A = Alloc(S, p1, ARENA_END)
        qdT = A("qdm", [4, SEQ], BF16)
        kdT = A("kdT", [2, SEQ], BF16)
        sgate = A("sgate", [2, SEQ], BF16)
        vtm = A("vtm", [NT, 2, 256], BF16)
        kdtok = A("kdtok", [NT, 256], BF16)
        t2s = [[A("bt%d_%d" % (k, i), [512], F32) for i in range(5)] for k in range(1)] * 2
        t2s = [[x[0], x[1], x[4], x[3], x[4], x[2]] for x in t2s]
        t = t2s[0]
        Ust = [[A("Ust%d_%d" % (p_, i), [64], F32, own=True) for i in range(2)] for p_ in range(2)]
        Sbf = [[A("Sbf%d_%d" % (p_, i), [64], BF16, own=True) for i in range(5)] for p_ in range(2)]
        Am = [A("bAm%d" % i, [512], BF16) for i in range(2)]
        oraw = [A("boraw%d" % i, [256], F32) for i in range(2)]
        osq = [A("bosq%d" % i, [256], BF16) for i in range(2)]
        zeros = A("bzeros", [128], BF16)
        S.memset("dve", zeros[:], 0.0)
        S.memset("pool", qdT[:], 0.0)
        S.memset("pool", vtm[:], 0.0)
        wf = WS.next("Bf")
        wq = WS.next("Bq")
        its = [(pr, tb) for pr in range(2) for tb in range(TT)]

        def prep1(k):
            pr, tb = its[k]
            tok = slice(tb * 512, (tb + 1) * 512)
            sg, ff, bb, eb, enb, qf = t2s[k % 2]
            bf_ = zbank()
            fm_matmul(bf_[:, :], wf, pr * 128, lambda kc: hT[:, kc, tok])
            bq_ = zbank()
            fm_matmul(bq_[:, :], wq, pr * 128, lambda kc: hT[:, kc, tok])
            S.act(sg[:], bf_[:, :], AF.Sigmoid)
            S.act(qf[:], bq_[:, :], AF.Silu)
            S.ts("dve", ff[:], sg[:], lbt[:, l, pr, 1:2], lbt[:, l, pr, 0:1], ALU.mult, ALU.add)

        def prep2(k):
            pr, tb = its[k]
            tok = slice(tb * 512, (tb + 1) * 512)
            sg, ff, bb, eb, enb, qf = t2s[k % 2]
            S.act(bb[:], ff[:], AF.Ln)
            S.op("dve", lambda e: e.tensor_tensor_scan(out=sg[:].ap, data0=cm[:, 1, :].ap, data1=bb[:].ap, initial=0.0,
                                                       op0=ALU.mult, op1=ALU.add),
                 reads=[cm[:, 1, :], bb[:]], writes=[sg[:]])
            S.ts("dve", sg[:], sg[:], -80.0, None, ALU.max)
            S.act(eb[:], sg[:], AF.Exp)
            S.act(enb[:], sg[:], AF.Exp, scale=-1.0)
            S.tt("dve", qdT[0:64, 2 * pr, tok], qf[0:64, :], eb[0:64, :], ALU.mult)
            S.tt("dve", qdT[64:128, 2 * pr + 1, tok], qf[64:128, :], eb[64:128, :], ALU.mult)
            S.ts("dve", ff[:], ff[:], -1.0, 1.0, ALU.mult, ALU.add)
            S.tt("dve", kdT[:, pr, tok], ff[:], enb[:], ALU.mult)
            ebs = Acc(eb.t[:, 63:512:64], S.arena, eb[:].slots)
            S.copy("dve", decs[:, pr, tb * 8:(tb + 1) * 8], ebs)

        for k in range(len(its)):
            prep1(k)
            prep2(k)
        wi_ = WS.next("Bi")
        for i in range(NT):
            bank = zbank()
            for kc in range(NKC):
                S.mm(bank[:, 0:256], hT[:, kc, i * 128:(i + 1) * 128], wi_[:, kc, :], start=(kc == 0), stop=(kc == NKC - 1))
            S.copy("act", vtm[0:64, i, 0, :], bank[0:64, 0:256])
            S.copy("dve", vtm[64:128, i, 1, :], bank[64:128, 0:256])
        wg = WS.next("Bg")
        for pr in range(2):
            for tb in range(TT):
                tok = slice(tb * 512, (tb + 1) * 512)
                bank = zbank()
                fm_matmul(bank[:, :], wg, pr * 128, lambda kc: hT[:, kc, tok])
                S.act(sgate[:, pr, tok], bank[:, :], AF.Silu)
        for i in range(NT):
            tb_ = PS[3 + i % 2]
            for pr in range(2):
                S.transpose(tb_.bf((slice(None), slice(pr * 128, (pr + 1) * 128))), kdT[:, pr, i * 128:(i + 1) * 128], ident[:])
            S.copy("act" if i % 2 else "dve", kdtok[:, i, :], tb_.bf((slice(None), slice(0, 256))))
        def coreA(i):
            tok = slice(i * 128, (i + 1) * 128)
            sb = PS[i % 2]
            for h in range(4):
                S.mm(sb[:, h * 128:(h + 1) * 128], kdT[:, h // 2, tok], qdT[:, h, tok])
            db = PS[4 + i % 2]
            for cc in range(2):
                for h in range(4):
                    off = 64 * (h % 2)
                    pr = h // 2
                    S.mm(db[off:off + 64, (cc * 2 + pr) * 64:(cc * 2 + pr + 1) * 64],
                         kdtok[:, i, h * 64:(h + 1) * 64],
                         vtm[:, i, cc, h * 64:(h + 1) * 64])
            am = Am[i % 2]
            S.tt("dve", am[:], sb[:, :], cm[:, 0, :], ALU.mult)
            for cc in range(2):
                n = 2 * i + cc
                for pr in range(2):
                    dl = db[:, (cc * 2 + pr) * 64:(cc * 2 + pr + 1) * 64]
                    if n == 0:
                        S.copy("dve", Ust[pr][n % 2][:], dl)
                    else:
                        S.stt("dve", Ust[pr][n % 2][:], Ust[pr][(n + 1) % 2][:], decs[:, pr, n - 1:n], dl, ALU.mult, ALU.add)
                    if n + 1 < NCH:
                        S.act(Sbf[pr][(n + 1) % 5][:], Ust[pr][n % 2][:], AF.Copy, scale=decs[:, pr, n:n + 1])

        def coreB(i):
            tok = slice(i * 128, (i + 1) * 128)
            am = Am[i % 2]
            ob = PS[2 + i % 2]
            S.mm(ob[:, 0:256], zeros[:], am[:, 0:256], start=True, stop=False, skip=True)
            for h in range(4):
                off = 64 * (h % 2)
                pr = h // 2
                for cc in range(2):
                    S.mm(ob[off:off + 64, pr * 128:(pr + 1) * 128], vtm[:, i, cc, h * 64:(h + 1) * 64], am[:, h * 128:(h + 1) * 128],
                         start=False, stop=False, skip=True)
            for cc in range(2):
                n = 2 * i + cc
                if n == 0:
                    continue
                for h in range(4):
                    off = 64 * (h % 2)
                    pr = h // 2
                    S.mm(ob[off:off + 64, pr * 128 + cc * 64:pr * 128 + (cc + 1) * 64],
                         Sbf[pr][n % 5][:],
                         qdT[:, h, i * 128 + cc * 64:i * 128 + (cc + 1) * 64], start=False, stop=False, skip=True)
            orw = oraw[i % 2]
            S.copy("act", orw[:], ob[:, 0:256])
            S.act(osq[i % 2][:], orw[:], AF.Square)
            nb = PS[6 + i % 2]
            S.mm(nb[:, 0:256], bones[:], osq[i % 2][:])
            lnv = t[0]
            rs = t[1]
            rstd_from_ss(rs[:, 0:256], nb[:, 0:256], 64, lnv[:, 0:256])
            for pr in range(2):
                S.stt("dve", orw[:, pr * 128:(pr + 1) * 128], orw[:, pr * 128:(pr + 1) * 128], pcol[:, l, 32 + pr:33 + pr],
                      rs[:, pr * 128:(pr + 1) * 128], ALU.mult, ALU.mult)
            o2 = Acc(orw.t[:].rearrange("p (a t) -> p a t", a=2), S.arena, orw[:].slots)
            S.tt("dve", o_all[:, 1, :, tok], o2, sgate[:, :, tok], ALU.mult)

        for i in range(NT + 1):
            if i < NT:
                coreA(i)
            if i >= 1:
                coreB(i - 1)

    def mixer_C(l):
        A = Alloc(S, p1, ARENA_END)
        uT = A("uT", [2, SEQ], BF16)
        vntok = A("vntok", [NT, 256], BF16)
        junk = A("cjunk", [256], F32)
        t2 = [A("ct%d" % i, [256], F32) for i in range(2)]
        S.dma("sp", gnorm[:].ap, gnorm_d[l], writes=[gnorm[:]])
        S.dma("sp", wsTf[:].ap, wsT_d[l], writes=[wsTf[:]])
        S.dma("sp", bst[:].ap, bst_d[l], writes=[bst[:]])
        S.tt("dve", wsTb[:], wsTf[:], Acc(cm.t[:, 2, 0:128].unsqueeze(1).to_broadcast([128, 4, 128]), S.arena, cm[:, 2, 0:128].slots),
             ALU.mult)
        wv = WS.next("Cv")
        vall = A("cvall", [NT, 256], F32)
        css = A("css", [3, NT], F32)
        S.memset("dve", css[:], 0.0)
        for i in range(NT):
            bank = zbank()
            for kc in range(NKC):
                S.mm(bank[:, 0:256], hT[:, kc, i * 128:(i + 1) * 128], wv[:, kc, :], start=(kc == 0), stop=(kc == NKC - 1))
            S.act(vall[:, i, :], bank[:, 0:256], AF.Gelu_apprx_tanh)
            S.act(junk[:], vall[:, i, :], AF.Square, accum=css[:, 0, i:i + 1])
        rstd_from_ss(css[:, 2, :], css[:, 0, :], MIXW, css[:, 1, :])
        for i in range(NT):
            S.stt("dve", vntok[:, i, :], vall[:, i, :], css[:, 2, i:i + 1], gnorm[:], ALU.mult, ALU.mult)
        wu = WS.next("Cu")
        for pr in range(2):
            for tb in range(TT):
                tok = slice(tb * 512, (tb + 1) * 512)
                bank = zbank()
                fm_matmul(bank[:, :], wu, pr * 128, lambda kc: hT[:, kc, tok])
                S.act(uT[:, pr, tok], bank[:, :], AF.Gelu_apprx_tanh)
        for i in range(NT):
            tok = slice(i * 128, (i + 1) * 128)
            mb = PS[3 + i % 2]
            for g in range(4):
                off = 64 * (g % 2)
                pr = g // 2
                S.mm(mb[off:off + 64, pr * 128:(pr + 1) * 128], vntok[:, i, g * 64:(g + 1) * 64], wsTb[:, g, :])
            tm = t2[i % 2]
            tm3 = Acc(tm.t[:].rearrange("p (a t) -> p a t", a=2), S.arena, tm[:].slots)
            mb3 = Acc(mb.t[:, 0:256].rearrange("p (a t) -> p a t", a=2), mb.space, (0,))
            S.tt("dve", tm3, mb3, bst[:], ALU.add)
            S.tt("dve", o_all[:, 2, :, tok], tm3, uT[:, :, tok], ALU.mult)

    def mixer_D(l):
        A = Alloc(S, p1, ARENA_END)
        xraw = A("xraw", [2, SEQ], F32)
        ggate = A("ggate", [2, SEQ], BF16)
        t2s = [[A("dt%d_%d" % (k, i), [512], F32) for i in range(8)] for k in range(2)]
        xcb = [A("dxcb%d" % i, [512], BF16) for i in range(2)]
        S.dma("sp", waxf[:].ap, wax_d[l], writes=[waxf[:]])
        S.copy("dve", waxb[:], waxf[:])
        wg = WS.next("Dg")
        for pr in range(2):
            for tb in range(TT):
                tok = slice(tb * 512, (tb + 1) * 512)
                bank = zbank()
                fm_matmul(bank[:, :], wg, pr * 128, lambda kc: hT[:, kc, tok])
                S.act(ggate[:, pr, tok], bank[:, :], AF.Gelu_apprx_tanh)
        wx = WS.next("Dx")
        for pr in range(2):
            for tb in range(TT):
                tok = slice(tb * 512, (tb + 1) * 512)
                bank = zbank()
                fm_matmul(bank[:, :], wx, pr * 128, lambda kc: hT[:, kc, tok])
                S.copy("act" if tb % 2 else "dve", xraw[:, pr, tok], bank[:, :])
        hh2 = [[A("dhp%d_%d" % (p_, i), [512], F32) for i in range(2)] for p_ in range(2)]
        its = [(pr, tb) for tb in range(TT) for pr in range(2)]

        def d1(k):
            pr, tb = its[k]
            t0 = tb * 512
            cw = lambda j: pcol[:, l, 36 + pr * 4 + j:37 + pr * 4 + j]
            xc, r_, ig, a_, a2, ml, bt_, _ = t2s[k % 2]
            S.ts("dve", xc[:], xraw[:, pr, t0:t0 + 512], cw(3), pcol[:, l, 34 + pr:35 + pr], ALU.mult, ALU.add)
            for j in range(3):
                sh = 3 - j
                lo = sh if tb == 0 else 0
                S.stt("dve", xc[:, lo:512], xraw[:, pr, t0 + lo - sh:t0 + 512 - sh], cw(j), xc[:, lo:512], ALU.mult, ALU.add)
            xb = xcb[k % 2]
            S.copy("act", xb[:], xc[:])
            ba_ = PS[3 + k % 2]
            bx_ = PS[5 + k % 2]
            S.mm(ba_[:, :], waxb[:, 0, pr, :], xb[:])
            S.mm(bx_[:, :], waxb[:, 1, pr, :], xb[:])
            S.act(r_[:], ba_[:, :], AF.Sigmoid, bias=pcol[:, l, 44 + pr:45 + pr])
            S.act(ig[:], bx_[:, :], AF.Sigmoid, bias=pcol[:, l, 46 + pr:47 + pr])

        def d2(k):
            pr, tb = its[k]
            t0 = tb * 512
            xc, r_, ig, a_, a2, ml, bt_, _ = t2s[k % 2]
            S.act(a_[:], r_[:], AF.Exp, scale=lruc[:, l, pr, 0:1])
            S.act(a2[:], r_[:], AF.Exp, scale=lruc[:, l, pr, 1:2])
            S.ts("dve", a2[:], a2[:], -1.0, 1.0, ALU.mult, ALU.add)
            S.act(ml[:], a2[:], AF.Sqrt)
            if tb == 0:
                S.memset("dve", ml[:, 0:1], 1.0)
            S.tt("dve", ig[:], ig[:], xc[:], ALU.mult)
            S.tt("dve", bt_[:], ml[:], ig[:], ALU.mult)
            h_ = hh2[pr][tb % 2]
            hp = hh2[pr][(tb + 1) % 2]
            if tb == 0:
                S.op("dve", lambda e: e.tensor_tensor_scan(out=h_[:].ap, data0=a_[:].ap, data1=bt_[:].ap, initial=0.0,
                                                           op0=ALU.mult, op1=ALU.add),
                     reads=[a_[:], bt_[:]], writes=[h_[:]])
            else:
                S.op("dve", lambda e: e.tensor_tensor_scan(out=h_[:].ap, data0=a_[:].ap, data1=bt_[:].ap,
                                                           initial=hp[:, 511:512].ap, op0=ALU.mult, op1=ALU.add),
                     reads=[a_[:], bt_[:], hp[:, 511:512]], writes=[h_[:]])
            S.tt("dve", o_all[:, 3, pr, t0:t0 + 512], h_[:], ggate[:, pr, t0:t0 + 512], ALU.mult)

        for k in range(len(its) + 1):
            if k < len(its):
                d1(k)
            if k >= 1:
                d2(k - 1)

    def block_phase(l, tb, last_layer):
        A = Alloc(S, p1, ARENA_END)
        tok = slice(tb * 512, (tb + 1) * 512)
        xt = A("xt", [NKC, 512], F32)
        yT = A("yT", [NKC, 512], F32)
        macc = A("macc", [NKC, 512], F32)
        mrg = A("mrg", [NKC, 512], BF16)
        hid = View(S, "hid", [NJ, 512], BF16, macc.off)
        sg = [A("sg%d" % i, [512], F32) for i in range(2)]
        tp = [A("tp%d" % i, [512], F32) for i in range(2)]
        A2 = Alloc(S, A.cur, ARENA_END)
        S.dma("sp", xt[:].ap, xT_d[:, :, tok], reads=[Acc(None, xT_space, (tb,))], writes=[xt[:]])
        k = 0
        for n in range(4):
            for q4 in range(4):
                wgt = WS.next("G")
                wb = WS.next("WB")
                for d2 in range(2):
                    dc = q4 * 2 + d2
                    gb = zbank()
                    fm_matmul(gb[:, :], wgt, d2 * 128, lambda kc: hT[:, kc, tok])
                    pb = zbank()
                    for kc in range(2):
                        S.mm(pb[:, :], wb[:, kc, d2 * 128:(d2 + 1) * 128], o_all[:, n, kc, tok], start=(kc == 0), stop=(kc == 1))
                    s_ = sg[k % 2]
                    S.act(s_[:], gb[:, :], AF.Sigmoid)
                    if n == 0:
                        S.tt("dve", macc[:, dc, :], s_[:], pb[:, :], ALU.mult)
                    else:
                        t_ = tp[k % 2]
                        S.tt("dve", t_[:], s_[:], pb[:, :], ALU.mult)
                        if n < 3:
                            S.tt("dve", macc[:, dc, :], macc[:, dc, :], t_[:], ALU.add)
                        else:
                            S.tt("dve", mrg[:, dc, :], macc[:, dc, :], t_[:], ALU.add)
                    k += 1
        ev = Evac(yT, 8, l, A2)
        for q4 in range(4):
            wo = WS.next("WO")
            for d2 in range(2):
                dc = q4 * 2 + d2
                yb = zbank()
                fm_matmul(yb[:, :], wo, d2 * 128, lambda kc: mrg[:, kc, :])
                ev.chunk(dc, yb)
        ev.finish()
        resid_norm(xt, yT, l, 16, tb, A2, ev.sq)
        for j2 in range(NJ // 2):
            w1g = WS.next("F1g")
            w1u = WS.next("F1u")
            for jj in range(2):
                j = j2 * 2 + jj
                gb = zbank()
                fm_matmul(gb[:, :], w1g, jj * 128, lambda kc: hT[:, kc, tok])
                ub = zbank()
                fm_matmul(ub[:, :], w1u, jj * 128, lambda kc: hT[:, kc, tok])
                s_ = sg[j % 2]
                S.act(s_[:], gb[:, :], AF.Silu)
                S.tt("dve", hid[:, j, :], s_[:], ub[:, :], ALU.mult)
        A2.cur = A.cur
        ev = Evac(yT, 24, l, A2)
        for dc in range(NKC):
            wa_ = WS.next("F2")
            wb_ = WS.next("F2")
            yb = zbank()
            fm_matmul(yb[:, :], wa_, 0, lambda kc: hid[:, kc, :], nk=11, k0=0, first=True, last=False)
            fm_matmul(yb[:, :], wb_, 0, lambda kc: hid[:, kc, :], nk=11, k0=11, first=False, last=True)
            ev.chunk(dc, yb)
        ev.finish()
        resid_norm(xt, yT, None if last_layer else l + 1, 0, tb, A2, ev.sq)
        if not last_layer:
            S.dma("sp", xT_d[:, :, tok], xt[:].ap, reads=[xt[:]], writes=[Acc(None, xT_space, (tb,))])
        else:
            A2.cur = A.cur
            otk = [A2("otk%d" % i, [D], F32) for i in range(2)]
            for i4 in range(4):
                ok = otk[i4 % 2]
                for half in range(2):
                    bank = PS[3 + (2 * i4 + half) % 2]
                    for c4 in range(4):
                        c = half * 4 + c4
                        S.transpose(bank[:, c4 * 128:(c4 + 1) * 128], xt[:, c, i4 * 128:(i4 + 1) * 128], identf[:])
                    S.copy("act" if half else "dve", ok[:, half * 512:(half + 1) * 512], bank[:, :])
                r0 = tb * 512 + i4 * 128
                S.dma("sp", out_d[r0:r0 + 128, :], ok[:].ap, reads=[ok[:]], is_output=True)

    def emit():
        if stop == "const":
            return
        load_input()
        if stop == "load":
            return
        for l in range(DEPTH):
            for nm, fn in (("A", mixer_A), ("B", mixer_B), ("C", mixer_C), ("D", mixer_D)):
                fn(l)
                if stop is not None and stop.startswith(nm):
                    return
            if dbg and l == 0:
                S.dma("sp", dbg_d, o_all[:].ap, reads=[o_all[:]], is_output=True)
            for tb in range(TT):
                block_phase(l, tb, l == DEPTH - 1)
    emit()
    if stop is not None and dbg:
        S.dma("sp", dbg_d, o_all[:].ap, reads=[o_all[:]], is_output=True)
    S.finish()
    return nc, S


def prep_shared(inp, DEPTH):
    f = lambda k: np.ascontiguousarray(np.asarray(inp[k], dtype=np.float32))
    pcol = np.zeros((DEPTH, 128, NPCOL), np.float32)

    def fm8(v):
        return v.reshape(DEPTH, 8, 128).transpose(0, 2, 1)

    def fm2(v):
        return v.reshape(DEPTH, 2, 128).transpose(0, 2, 1)

    pcol[:, :, 0:8] = fm8(f("norm_mix_pre"))
    pcol[:, :, 8:16] = fm8(f("norm_mix_post"))
    pcol[:, :, 16:24] = fm8(f("norm_ffn_pre"))
    pcol[:, :, 24:32] = fm8(f("norm_ffn_post"))
    pcol[:, :, 32:34] = fm2(f("hgrn_norm_g"))
    pcol[:, :, 34:36] = fm2(f("lru_conv_b"))
    cw = f("lru_conv_w")
    for pr in range(2):
        for j in range(4):
            pcol[:, :, 36 + pr * 4 + j] = cw[:, j, pr * 128:(pr + 1) * 128]
    pcol[:, :, 44:46] = fm2(f("lru_ba"))
    pcol[:, :, 46:48] = fm2(f("lru_bx"))
    pcol[:, :, 48:50] = fm2(f("lru_lambda"))
    lg = f("hgrn_lb_logits")
    pcol[:, :, 50:58] = NEG
    for pr in range(2):
        for d in range(DEPTH):
            pcol[:, :, 50 + pr * 4 + d] = lg[d, pr * 128:(pr + 1) * 128][None, :]
    gnorm_bc = np.ascontiguousarray(np.broadcast_to(f("gmlp_norm_g")[:, None, :], (DEPTH, 128, MIXW)))
    wsT = np.ascontiguousarray(f("gmlp_ws").transpose(0, 3, 1, 2))
    bs = f("gmlp_bs")
    bs_t = np.zeros((DEPTH, 128, 2, 128), np.float32)
    for g in range(4):
        bs_t[:, (g % 2) * 64:(g % 2) * 64 + 64, g // 2, :] = bs[:, g, None, :]
    wax = np.zeros((DEPTH, 128, 2, 2, 128), np.float32)
    for ai, key in enumerate(("lru_wa", "lru_wx")):
        w = f(key)
        for h in range(4):
            o = (h % 2) * 64
            wax[:, o:o + 64, ai, h // 2, o:o + 64] = w[:, h]
    rb = f("attn_rel_bias")
    k = np.arange(128)[:, None]
    q = np.arange(128)[None, :]
    abias = np.zeros((DEPTH, 128, 5, 4, 128), np.float32)
    for j in range(5):
        dist = (4 - j) * 128 + q - k
        idx = np.clip(dist, -63, 256) + 63
        cq = 8 + q // 64
        ck = 2 * j + k // 64
        valid = (ck >= cq - 8) & (ck <= cq)
        g = rb[:, :, idx]
        g = np.where(valid[None, None], g, np.float32(NEG))
        abias[:, :, j] = g.transpose(0, 2, 1, 3)
    cm = np.zeros((128, 4, 512), np.float32)
    s = np.arange(128)[:, None]
    t = np.arange(128)[None, :]
    m0 = ((s // 64) == (t // 64)) & (s <= t)
    cm[:, 0, :] = np.tile(m0.astype(np.float32), (1, 4))
    cm[:, 1, :] = (np.arange(512) % 64 != 0).astype(np.float32)[None, :]
    cm[:, 2, 0:128] = (s <= t).astype(np.float32)
    cm[:, 3, 0:128] = np.eye(128, dtype=np.float32)
    return {
        "w_in": f("w_in"), "w_branch": f("w_branch"), "w_out": f("w_out"),
        "w_ffn_in": f("w_ffn_in"), "w_ffn_out": f("w_ffn_out"),
        "pcol": pcol, "gnorm_bc": gnorm_bc, "wsT": wsT, "bs_t": bs_t, "wax": wax,
        "abias": abias, "cmasks": cm,
    }


_CACHE = {}


def kernel(**inputs):
    x = np.asarray(inputs["x"], dtype=np.float32)
    B, SEQ, _ = x.shape
    DEPTH = inputs["w_in"].shape[0]
    key = (SEQ, DEPTH)
    if key not in _CACHE:
        _CACHE[key] = build_program(SEQ, DEPTH)[0]
    nc = _CACHE[key]
    shared = prep_shared(inputs, DEPTH)
    in_maps = []
    for b in range(B):
        m = dict(shared)
        m["x"] = np.ascontiguousarray(x[b])
        in_maps.append(m)
    res = run_bass_kernel_spmd(nc, in_maps, core_ids=list(range(B)))
    return np.stack([np.asarray(r["out"], dtype=np.float32) for r in res.results], axis=0)
```

```python
import itertools
import numpy as np
import concourse.bass as bass
import concourse.mybir as mybir
from concourse.bass_utils import run_bass_kernel_spmd

F32 = mybir.dt.float32
BF16 = mybir.dt.bfloat16
AF = mybir.ActivationFunctionType
ALU = mybir.AluOpType
AX = mybir.AxisListType

D = 1024
NKC = 8
MIXW = 256
IN_COLS = 6912
FFH = 2816
NJ = 22
EPS = 1e-6
NEG = -30000.0
SLOT = 1024
NPCOL = 64


class Space:
    def __init__(self, n):
        self.lastw = [None] * n
        self.reads = [dict() for _ in range(n)]


class Acc:
    __slots__ = ("ap", "space", "slots")

    def __init__(self, ap, space, slots):
        self.ap = ap
        self.space = space
        self.slots = slots


class View:
    def __init__(self, S, name, fshape, dtype, off):
        self.S = S
        self.fshape = list(fshape)
        self.esz = mybir.dt.size(dtype)
        self.off = off
        self.nbytes = int(np.prod(fshape)) * self.esz
        assert off % 32 == 0 and off + self.nbytes <= S.arena_bytes, (name, off, self.nbytes)
        S.nview += 1
        self.t = S.nc.alloc_sbuf_tensor_at("%s_%d" % (name, S.nview), [128] + self.fshape, dtype,
                                           offset=S.arena_base + off)
        st = [1] * len(fshape)
        for i in range(len(fshape) - 2, -1, -1):
            st[i] = st[i + 1] * fshape[i + 1]
        self.strides = st

    def __getitem__(self, key):
        if not isinstance(key, tuple):
            key = (key,)
        fk = list(key[1:]) + [slice(None)] * (len(self.fshape) - (len(key) - 1))
        rng = []
        for k, n in zip(fk, self.fshape):
            if isinstance(k, int):
                rng.append((k, k + 1))
            else:
                lo = 0 if k.start is None else k.start
                hi = n if k.stop is None else k.stop
                rng.append((lo, hi))
        nd = len(rng)
        last = nd - 1
        while last > 0 and rng[last] == (0, self.fshape[last]):
            last -= 1
        slots = set()
        inner = self.strides[last]
        for idx in itertools.product(*[range(a, b) for a, b in rng[:last]]):
            base = sum(i * s for i, s in zip(idx, self.strides))
            b0 = self.off + (base + rng[last][0] * inner) * self.esz
            b1 = self.off + (base + rng[last][1] * inner) * self.esz
            slots.update(range(b0 // SLOT, (b1 - 1) // SLOT + 1))
        return Acc(self.t[key], self.S.arena, sorted(slots))


class PBank:
    def __init__(self, S, i):
        self.t = S.nc.alloc_psum_tensor("psb%d" % i, [128, 512], F32)
        self.space = Space(1)

    def __getitem__(self, key):
        return Acc(self.t[key], self.space, (0,))

    def bf(self, key):
        return Acc(self.t[:].bitcast(BF16)[key], self.space, (0,))


class Sched:
    ENG = ("pe", "dve", "act", "pool", "sp")

    def __init__(self, nc, arena_kib=206, n_dma_sems=6):
        self.nc = nc
        self.e = {"pe": nc.tensor, "dve": nc.vector, "act": nc.scalar,
                  "pool": nc.gpsimd, "sp": nc.sync}
        self.sem = {k: nc.alloc_semaphore("s_" + k) for k in self.ENG}
        self.cnt = {k: 0 for k in self.ENG}
        self.dsem, self.dcnt, self.drr = {}, {}, {}
        for q in ("sp", "pool"):
            self.dsem[q] = [nc.alloc_semaphore("d_%s%d" % (q, i)) for i in range(n_dma_sems)]
            self.dcnt[q] = [0] * n_dma_sems
            self.drr[q] = 0
        self.waited = {k: {} for k in self.ENG}
        self.out_waits = []
        self.nview = 0
        self.arena_bytes = arena_kib * 1024
        nc.alloc_sbuf_tensor("arena", [128, self.arena_bytes], mybir.dt.uint8)
        self.arena_base = list(nc.allocations)[-1].memorylocations[0].addr
        self.arena = Space(self.arena_bytes // SLOT)
        self.banks = [PBank(self, i) for i in range(8)]
        self.nops = 0

    def _semof(self, key):
        if isinstance(key, str):
            return self.sem[key]
        return self.dsem[key[0]][key[1]]

    def _wait(self, eng, key, val):
        w = self.waited[eng]
        if w.get(key, 0) >= val:
            return
        self.e[eng].wait_ge(self._semof(key), val)
        w[key] = val

    def _deps(self, eng, reads, writes):
        deps = {}
        for a in reads:
            sp = a.space
            for s in a.slots:
                d = sp.lastw[s]
                if d is not None and deps.get(d[0], 0) < d[1]:
                    deps[d[0]] = d[1]
        for a in writes:
            sp = a.space
            for s in a.slots:
                d = sp.lastw[s]
                if d is not None and deps.get(d[0], 0) < d[1]:
                    deps[d[0]] = d[1]
                for k, v in sp.reads[s].items():
                    if deps.get(k, 0) < v:
                        deps[k] = v
        for k, v in deps.items():
            if k == eng and eng == "pe":
                continue
            self._wait(eng, k, v)

    def _record(self, key, val, reads, writes):
        for a in reads:
            sp = a.space
            for s in a.slots:
                sp.reads[s][key] = val
        for a in writes:
            sp = a.space
            for s in a.slots:
                sp.lastw[s] = (key, val)
                sp.reads[s] = {}

    def op(self, eng, fn, reads=(), writes=()):
        self._deps(eng, reads, writes)
        ins = fn(self.e[eng])
        self.cnt[eng] += 1
        ins.then_inc(self.sem[eng], 1)
        self._record(eng, self.cnt[eng], reads, writes)
        self.nops += 1
        return ins

    def dma(self, q, out, in_, reads=(), writes=(), is_output=False):
        i = self.drr[q]
        self.drr[q] = (i + 1) % len(self.dsem[q])
        key = (q, i)
        if self.dcnt[q][i] > 0:
            self._wait(q, key, self.dcnt[q][i])
        self._deps(q, reads, writes)
        ins = self.e[q].dma_start(out=out, in_=in_)
        self.dcnt[q][i] += 16
        ins.then_inc(self.dsem[q][i], 16)
        self._record(key, self.dcnt[q][i], reads, writes)
        if is_output:
            self.out_waits.append((key, self.dcnt[q][i]))
        self.nops += 1
        return ins

    def finish(self):
        for key, v in self.out_waits:
            self._wait("sp", key, v)
        for k in ("pe", "dve", "act", "pool"):
            if self.cnt[k]:
                self._wait("sp", k, self.cnt[k])

    def mm(self, out, lhsT, rhs, start=True, stop=True, skip=False):
        return self.op("pe", lambda e: e.matmul(out.ap, lhsT=lhsT.ap, rhs=rhs.ap, start=start, stop=stop,
                                                skip_group_check=skip),
                       reads=[lhsT, rhs], writes=[out])

    def transpose(self, out, in_, ident):
        return self.op("pe", lambda e: e.transpose(out.ap, in_.ap, ident.ap), reads=[in_, ident], writes=[out])

    def act(self, out, in_, func, scale=None, bias=None, accum=None, eng="act"):
        kw = {}
        rd = [in_]
        wr = [out]
        if scale is not None:
            if isinstance(scale, Acc):
                rd.append(scale)
                kw["scale"] = scale.ap
            else:
                kw["scale"] = scale
        if bias is not None:
            if isinstance(bias, Acc):
                rd.append(bias)
                kw["bias"] = bias.ap
            else:
                kw["bias"] = bias
        if accum is not None:
            wr.append(accum)
            kw["accum_out"] = accum.ap
        return self.op(eng, lambda e: e.activation(out=out.ap, in_=in_.ap, func=func, **kw), reads=rd, writes=wr)

    def tt(self, eng, out, in0, in1, op):
        return self.op(eng, lambda e: e.tensor_tensor(out=out.ap, in0=in0.ap, in1=in1.ap, op=op),
                       reads=[in0, in1], writes=[out])

    def ts(self, eng, out, in0, s1, s2, op0, op1=None):
        rd = [in0]
        a1 = s1.ap if isinstance(s1, Acc) else s1
        a2 = s2.ap if isinstance(s2, Acc) else s2
        if isinstance(s1, Acc):
            rd.append(s1)
        if isinstance(s2, Acc):
            rd.append(s2)
        if op1 is None:
            return self.op(eng, lambda e: e.tensor_scalar(out=out.ap, in0=in0.ap, scalar1=a1, scalar2=None, op0=op0),
                           reads=rd, writes=[out])
        return self.op(eng, lambda e: e.tensor_scalar(out=out.ap, in0=in0.ap, scalar1=a1, scalar2=a2, op0=op0, op1=op1),
                       reads=rd, writes=[out])

    def stt(self, eng, out, in0, scalar, in1, op0, op1):
        rd = [in0, in1]
        sc = scalar.ap if isinstance(scalar, Acc) else scalar
        if isinstance(scalar, Acc):
            rd.append(scalar)
        return self.op(eng, lambda e: e.scalar_tensor_tensor(out=out.ap, in0=in0.ap, scalar=sc, in1=in1.ap, op0=op0, op1=op1),
                       reads=rd, writes=[out])

    def copy(self, eng, out, in_):
        if eng == "act":
            return self.act(out, in_, AF.Copy)
        return self.op(eng, lambda e: e.tensor_copy(out=out.ap, in_=in_.ap), reads=[in_], writes=[out])

    def memset(self, eng, out, val):
        return self.op(eng, lambda e: e.memset(out.ap, val), writes=[out])


def bcast(acc, shape):
    return Acc(acc.ap.unsqueeze(len(acc.ap.shape)).to_broadcast(list(shape)), acc.space, acc.slots)


class Alloc:
    def __init__(self, S, lo, hi):
        self.S, self.lo, self.hi, self.cur = S, lo, hi, lo

    def __call__(self, name, fshape, dtype, own=False):
        n = int(np.prod(fshape)) * mybir.dt.size(dtype)
        al = SLOT if (n >= SLOT or own) else 64
        off = (self.cur + al - 1) // al * al
        assert off + n <= self.hi, ("arena region overflow", name, off, n, self.hi)
        self.cur = off + n
        if n >= SLOT or own:
            self.cur = (self.cur + SLOT - 1) // SLOT * SLOT
        return View(self.S, name, fshape, dtype, off)


class WStream:
    NSLOT = 8
    SLOT_EL = 2048
    AHEAD = 5

    def __init__(self, S, alloc, plan):
        self.S = S
        self.plan = plan
        self.slots = [alloc("wslot%d" % i, [self.SLOT_EL], BF16) for i in range(self.NSLOT)]
        self.views = {}
        self.issued = 0
        self.k = 0

    def _view(self, si, a, b):
        key = (si, a, b)
        if key not in self.views:
            sl = self.slots[si]
            self.views[key] = View(self.S, "wv", [a, b], BF16, sl.off)
        return self.views[key]

    def _issue(self):
        tag, src, (a, b) = self.plan[self.issued]
        v = self._view(self.issued % self.NSLOT, a, b)
        self.S.dma("pool", v[:].ap, src, writes=[v[:]])
        self.issued += 1

    def next(self, tag):
        k = self.k
        assert self.plan[k][0] == tag, (k, self.plan[k][0], tag)
        while self.issued < min(len(self.plan), k + self.AHEAD + 1):
            self._issue()
        self.k += 1
        _, _, (a, b) = self.plan[k]
        return self._view(k % self.NSLOT, a, b)


def build_program(SEQ, DEPTH, dbg=False, stop=None):
    NT = SEQ // 128
    TT = SEQ // 512
    NCH = SEQ // 64
    nc = bass.Bass("TRN2", target_bir_lowering=False)
    S = Sched(nc)
    PS = S.banks

    def dram(name, shape, dt=F32, kind="ExternalInput"):
        return nc.dram_tensor(name, list(shape), dt, kind=kind).ap()

    x_in = dram("x", [SEQ, D])
    w_in = dram("w_in", [DEPTH, D, IN_COLS])
    w_br = dram("w_branch", [DEPTH, 4, MIXW, D])
    w_out = dram("w_out", [DEPTH, D, D])
    w_f1 = dram("w_ffn_in", [DEPTH, D, 2 * FFH])
    w_f2 = dram("w_ffn_out", [DEPTH, FFH, D])
    pcol_d = dram("pcol", [DEPTH, 128, NPCOL])
    gnorm_d = dram("gnorm_bc", [DEPTH, 128, MIXW])
    wsT_d = dram("wsT", [DEPTH, 128, 4, 128])
    bst_d = dram("bs_t", [DEPTH, 128, 2, 128])
    wax_d = dram("wax", [DEPTH, 128, 2, 2, 128])
    abias_d = dram("abias", [DEPTH, 128, 5, 4, 128])
    cmask_d = dram("cmasks", [128, 4, 512])
    out_d = dram("out", [SEQ, D], kind="ExternalOutput")
    xT_d = dram("xT_scratch", [128, NKC, SEQ], kind="Internal")
    xT_space = Space(TT)
    if dbg:
        dbg_d = dram("dbg_oall", [128, 4, 2, SEQ], BF16, kind="ExternalOutput")

    KB = 1024
    A_const = Alloc(S, 0, 24 * KB)
    A_w = Alloc(S, 24 * KB, 56 * KB)
    A_h = Alloc(S, 56 * KB, 56 * KB + 16 * SEQ)
    p0 = 56 * KB + 16 * SEQ
    A_o = Alloc(S, p0, p0 + 16 * SEQ)
    p1 = p0 + 16 * SEQ
    ARENA_END = S.arena_bytes

    hT = A_h("hT", [NKC, SEQ], BF16)
    o_all = A_o("o_all", [4, 2, SEQ], BF16)

    ident = A_const("ident", [128], BF16)
    ones_bf = A_const("ones", [128], BF16)
    bones = A_const("bones", [128], BF16)
    cm = A_const("cmasks", [4, 512], F32)
    pcol = A_const("pcol", [DEPTH, NPCOL], F32)
    lbt = A_const("lbt", [DEPTH, 2, 2], F32)
    lruc = A_const("lruc", [DEPTH, 2, 2], F32)
    tmpc = A_const("tmpc", [32], F32)
    gnorm = A_const("gnorm", [MIXW], F32)
    wsTf = A_const("wsTf", [4, 128], F32)
    wsTb = A_const("wsTb", [4, 128], BF16)
    bst = A_const("bst", [2, 128], F32)
    waxf = A_const("waxf", [2, 2, 128], F32)
    waxb = A_const("waxb", [2, 2, 128], BF16)
    decs = A_const("decs", [2, NCH], F32)
    rcol = A_const("rcol", [16], F32)

    def wsrc(w2d, c0, ncols, k0=0, nk=NKC):
        return w2d[k0 * 128:(k0 + nk) * 128, c0:c0 + ncols].rearrange("(kc p) c -> p kc c", p=128)

    plan = []
    for l in range(DEPTH):
        wi = w_in[l]
        plan.append(("Aq", wsrc(wi, 0, 256), (NKC, 256)))
        plan.append(("Ak", wsrc(wi, 256, 256), (NKC, 256)))
        plan.append(("Av", wsrc(wi, 512, 256), (NKC, 256)))
        plan.append(("Bf", wsrc(wi, 1024, 256), (NKC, 256)))
        plan.append(("Bq", wsrc(wi, 768, 256), (NKC, 256)))
        plan.append(("Bi", wsrc(wi, 1280, 256), (NKC, 256)))
        plan.append(("Bg", wsrc(wi, 1536, 256), (NKC, 256)))
        plan.append(("Cv", wsrc(wi, 2048, 256), (NKC, 256)))
        plan.append(("Cu", wsrc(wi, 1792, 256), (NKC, 256)))
        plan.append(("Dg", wsrc(wi, 2560, 256), (NKC, 256)))
        plan.append(("Dx", wsrc(wi, 2304, 256), (NKC, 256)))
        for tb in range(TT):
            for n in range(4):
                for q4 in range(4):
                    plan.append(("G", wsrc(wi, 2816 + n * 1024 + q4 * 256, 256), (NKC, 256)))
                    plan.append(("WB", wsrc(w_br[l, n], q4 * 256, 256, 0, 2), (2, 256)))
            for q4 in range(4):
                plan.append(("WO", wsrc(w_out[l], q4 * 256, 256), (NKC, 256)))
            for j2 in range(NJ // 2):
                plan.append(("F1g", wsrc(w_f1[l], j2 * 256, 256), (NKC, 256)))
                plan.append(("F1u", wsrc(w_f1[l], FFH + j2 * 256, 256), (NKC, 256)))
            for dc in range(NKC):
                for kh in range(2):
                    plan.append(("F2", wsrc(w_f2[l], dc * 128, 128, kh * 11, 11), (11, 128)))
    WS = WStream(S, A_w, plan)

    S.dma("sp", cm[:].ap, cmask_d, writes=[cm[:]])
    S.dma("sp", pcol[:].ap, pcol_d.rearrange("l p c -> p l c"), writes=[pcol[:]])
    S.memset("dve", ones_bf[:], 1.0)
    S.memset("dve", bones[:], 0.0)
    S.memset("dve", bones[0:64, 0:64], 1.0)
    S.memset("dve", bones[64:128, 64:128], 1.0)
    class _IdF:
        def __getitem__(self, key):
            return cm[:, 3, 0:128]
    identf = _IdF()
    S.copy("dve", ident[:], identf[:])
    lg = pcol[:, 0, 50:58]
    S.act(tmpc[:, 0:8], lg, AF.Exp)
    e3 = Acc(tmpc[:, 0:8].ap.rearrange("p (a d) -> p a d", a=2), S.arena, tmpc[:, 0:8].slots)
    S.op("dve", lambda e: e.reduce_sum(out=tmpc[:, 8:10].ap, in_=e3.ap, axis=AX.X), reads=[e3], writes=[tmpc[:, 8:10]])
    S.op("dve", lambda e: e.reciprocal(out=tmpc[:, 10:12].ap, in_=tmpc[:, 8:10].ap), reads=[tmpc[:, 8:10]], writes=[tmpc[:, 10:12]])
    for pr in range(2):
        S.ts("dve", tmpc[:, 12 + pr * 4:16 + pr * 4], tmpc[:, pr * 4:pr * 4 + 4], tmpc[:, 10 + pr:11 + pr], None, ALU.mult)
    for l in range(DEPTH):
        for pr in range(2):
            if l == 0:
                S.memset("dve", lbt[:, l, pr, 0:1], 0.0)
            else:
                S.tt("dve", lbt[:, l, pr, 0:1], lbt[:, l - 1, pr, 0:1], tmpc[:, 12 + pr * 4 + l:13 + pr * 4 + l], ALU.add)
            S.ts("dve", lbt[:, l, pr, 1:2], lbt[:, l, pr, 0:1], -1.0, 1.0, ALU.mult, ALU.add)
    for l in range(DEPTH):
        S.act(tmpc[:, 20:22], pcol[:, l, 48:50], AF.Exp, scale=-1.0)
        S.act(tmpc[:, 22:24], tmpc[:, 20:22], AF.Ln, bias=1.0)
        for pr in range(2):
            S.ts("dve", lruc[:, l, pr, 0:1], tmpc[:, 22 + pr:23 + pr], -8.0, None, ALU.mult)
            S.ts("dve", lruc[:, l, pr, 1:2], tmpc[:, 22 + pr:23 + pr], -16.0, None, ALU.mult)

    zring = [0]

    def zbank():
        b = PS[zring[0] % 3]
        zring[0] += 1
        return b

    def fm_matmul(bank, wv, c0, rhs_fn, nk=NKC, k0=0, first=True, last=True, ncols=128):
        for kc in range(nk):
            S.mm(bank, wv[:, kc, c0:c0 + ncols], rhs_fn(k0 + kc),
                 start=(first and kc == 0), stop=(last and kc == nk - 1))

    def rstd_from_ss(out, ss_bank, n, tmp):
        S.act(tmp, ss_bank, AF.Ln, scale=1.0 / n, bias=EPS)
        S.act(out, tmp, AF.Exp, scale=-0.5)

    def norm_to_hT(xt, gcol0, l, tb, A):
        sq = A("nsq", [NKC, 512], BF16)
        lnv = A("nln", [512], F32)
        rs = A("nrs", [512], F32)
        bank = PS[7]
        for c in range(NKC):
            S.tt("pool", sq[:, c, :], xt[:, c, :], xt[:, c, :], ALU.mult)
        for c in range(NKC):
            S.mm(bank[:, :], ones_bf[:], sq[:, c, :], start=(c == 0), stop=(c == NKC - 1))
        rstd_from_ss(rs[:], bank[:, :], D, lnv[:])
        for c in range(NKC):
            S.stt("dve", hT[:, c, tb * 512:(tb + 1) * 512], xt[:, c, :], pcol[:, l, gcol0 + c:gcol0 + c + 1], rs[:],
                  ALU.mult, ALU.mult)

    class Evac:
        def __init__(self, yT, gcol0, l, A):
            self.yT, self.g0, self.l = yT, gcol0, l
            self.sq = A("esq", [NKC, 512], BF16)
            self.pending = None

        def _ss(self, c):
            S.mm(PS[7][:, :], ones_bf[:], self.sq[:, c, :], start=(c == 0), stop=(c == NKC - 1))

        def chunk(self, dc, bank):
            S.act(self.yT[:, dc, :], bank[:, :], AF.Copy, scale=pcol[:, self.l, self.g0 + dc:self.g0 + dc + 1])
            S.act(self.sq[:, dc, :], bank[:, :], AF.Square)
            if self.pending is not None:
                self._ss(self.pending)
            self.pending = dc

        def finish(self):
            self._ss(self.pending)

    def resid_norm(xt, yT, l_next, gcol_next, tb, A, sq):
        lnv = A("rln", [512], F32)
        rs = A("rrs", [512], F32)
        rstd_from_ss(rs[:], PS[7][:, :], D, lnv[:])
        for c in range(NKC):
            S.tt("dve", yT[:, c, :], yT[:, c, :], rs[:], ALU.mult)
            S.tt("dve", xt[:, c, :], xt[:, c, :], yT[:, c, :], ALU.add)
            if l_next is not None:
                S.act(sq[:, c, :], xt[:, c, :], AF.Square)
        if l_next is None:
            return
        bank = PS[6]
        for c in range(NKC):
            S.mm(bank[:, :], ones_bf[:], sq[:, c, :], start=(c == 0), stop=(c == NKC - 1))
        lnv2 = A("rln2", [512], F32)
        rs2 = A("rrs2", [512], F32)
        rstd_from_ss(rs2[:], bank[:, :], D, lnv2[:])
        for c in range(NKC):
            S.stt("dve", hT[:, c, tb * 512:(tb + 1) * 512], xt[:, c, :], pcol[:, l_next, gcol_next + c:gcol_next + c + 1], rs2[:],
                  ALU.mult, ALU.mult)

    def load_input():
        A = Alloc(S, p1, ARENA_END)
        xtok = [A("xtok%d" % i, [D], F32) for i in range(2)]
        xts = [A("xt%d" % i, [NKC, 512], F32) for i in range(2)]
        for tb in range(TT):
            xt = xts[tb % 2]
            for i4 in range(4):
                i = tb * 4 + i4
                xk = xtok[i % 2]
                S.dma("sp", xk[:].ap, x_in[i * 128:(i + 1) * 128, :], writes=[xk[:]])
                for half in range(2):
                    bank = PS[3 + (2 * i + half) % 2]
                    for c4 in range(4):
                        c = half * 4 + c4
                        S.transpose(bank[:, c4 * 128:(c4 + 1) * 128], xk[:, c * 128:(c + 1) * 128], identf[:])
                    src = Acc(bank.t[:].rearrange("p (c t) -> p c t", c=4), bank.space, (0,))
                    S.copy("act" if half else "dve", xt[:, half * 4:(half + 1) * 4, i4 * 128:(i4 + 1) * 128], src)
            S.dma("sp", xT_d[:, :, tb * 512:(tb + 1) * 512], xt[:].ap, reads=[xt[:]],
                  writes=[Acc(None, xT_space, (tb,))])
            A2 = Alloc(S, A.cur, ARENA_END)
            norm_to_hT(xt, 0, 0, tb, A2)

    def mixer_A(l):
        A = Alloc(S, p1, ARENA_END)
        qT = A("qTm", [4, SEQ], BF16)
        kT = A("kT", [2, SEQ], BF16)
        vaug = A("vaug", [NT, 4, 65], BF16)
        ab = A("abias", [5, 4, 128], F32)
        SK = 3
        PT = [A("aPT%d" % i, [512], BF16) for i in range(SK + 1)]
        otok = [A("aotok%d" % i, [256], BF16, own=True) for i in range(2)]
        rec = [A("arec%d" % i, [4], F32, own=True) for i in range(2)]
        S.dma("sp", ab[:].ap, abias_d[l], writes=[ab[:]])
        abb = A("abias_bf", [5, 512], BF16)
        S.copy("dve", abb[:], Acc(ab.t[:].rearrange("p j h q -> p j (h q)"), S.arena, ab[:].slots))
        S.memset("dve", vaug[:], 1.0)
        S.memset("pool", qT[:], 0.0)
        for c in range(4):
            if c % 2 == 0:
                wqk = WS.next("Aq" if c == 0 else "Ak")
            for tb in range(TT):
                bank = zbank()
                fm_matmul(bank[:, :], wqk, (c % 2) * 128, lambda kc: hT[:, kc, tb * 512:(tb + 1) * 512])
                if c < 2:
                    S.act(qT[0:64, 2 * c, tb * 512:(tb + 1) * 512], bank[0:64, :], AF.Copy, scale=0.125)
                    S.act(qT[64:128, 2 * c + 1, tb * 512:(tb + 1) * 512], bank[64:128, :], AF.Copy, scale=0.125)
                else:
                    S.copy("dve", kT[:, c - 2, tb * 512:(tb + 1) * 512], bank[:, :])
        if stop == "A1":
            return
        wv = WS.next("Av")
        for i in range(NT):
            bank = zbank()
            for kc in range(NKC):
                S.mm(bank[:, 0:256], hT[:, kc, i * 128:(i + 1) * 128], wv[:, kc, :], start=(kc == 0), stop=(kc == NKC - 1))
            src = Acc(bank.t[:, 0:256].rearrange("p (h d) -> p h d", h=4), bank.space, (0,))
            S.copy("act" if i % 2 else "dve", vaug[:, i, :, 0:64], src)
        if stop == "A2":
            return
        steps = [(m, kt) for m in range(NT) for kt in range(max(0, m - 4), m + 1)]

        def qk(si):
            m, kt = steps[si]
            bank = PS[si % (SK + 1)]
            j = kt - m + 4
            S.mm(bank[:, :], ident[:], abb[:, j, :], start=True, stop=False, skip=True)
            for h in range(4):
                S.mm(bank[:, h * 128:(h + 1) * 128], kT[:, h // 2, kt * 128:(kt + 1) * 128],
                     qT[:, h, m * 128:(m + 1) * 128], start=False, stop=(h == 3), skip=True)
            S.act(PT[si % (SK + 1)][:], bank[:, :], AF.Exp)

        def pv(si):
            m, kt = steps[si]
            first = kt == max(0, m - 4)
            last = kt == m
            ob = PS[4 + m % 2]
            for h in range(4):
                S.mm(ob[:, h * 65:(h + 1) * 65], PT[si % (SK + 1)][:, h * 128:(h + 1) * 128], vaug[:, kt, h, :],
                     start=(first and h == 0), stop=last, skip=True)
            if last:
                o3 = Acc(ob.t[:, 0:260].rearrange("p (h d) -> p h d", h=4), ob.space, (0,))
                rc = rec[m % 2]
                S.op("dve", lambda e: e.reciprocal(out=rc[:].ap, in_=o3.ap[:, :, 64]), reads=[o3], writes=[rc[:]])
                ot = otok[m % 2]
                o3v = Acc(o3.ap[:, :, 0:64], ob.space, (0,))
                otv = Acc(ot.t[:].rearrange("p (h d) -> p h d", h=4), S.arena, ot[:].slots)
                S.tt("dve", otv, o3v, bcast(rc[:], [128, 4, 64]), ALU.mult)
                tb_ = PS[6 + m % 2]
                for pr in range(2):
                    S.transpose(tb_.bf((slice(None), slice(pr * 128, (pr + 1) * 128))), ot[:, pr * 128:(pr + 1) * 128], ident[:])
                srcT = Acc(tb_.t[:].bitcast(BF16)[:, 0:256].rearrange("p (a t) -> p a t", a=2), tb_.space, (0,))
                S.copy("act", o_all[:, 0, :, m * 128:(m + 1) * 128], srcT)

        for si in range(len(steps) + SK):
            if si < len(steps):
                qk(si)
            if si >= SK:
                pv(si - SK)

    def mixer_B(l):
        A = Alloc(S, p1, ARENA_END)
        qdT = A("qdm", [4, SEQ], BF16)
        kdT = A("kdT", [2, SEQ], BF16)
        sgate = A("sgate", [2, SEQ], BF16)
        vtm = A("vtm", [NT, 2, 256], BF16)
        kdtok = A("kdtok", [NT, 256], BF16)
        t2s = [[A("bt%d_%d" % (k, i), [512], F32) for i in range(5)] for k in range(1)] * 2
        t2s = [[x[0], x[1], x[4], x[3], x[4], x[2]] for x in t2s]
        t = t2s[0]
        Ust = [[A("Ust%d_%d" % (p_, i), [64], F32, own=True) for i in range(2)] for p_ in range(2)]
        Sbf = [[A("Sbf%d_%d" % (p_, i), [64], BF16, own=True) for i in range(5)] for p_ in range(2)]
        Am = [A("bAm%d" % i, [512], BF16) for i in range(2)]
        oraw = [A("boraw%d" % i, [256], F32) for i in range(2)]
        osq = [A("bosq%d" % i, [256], BF16) for i in range(2)]
        zeros = A("bzeros", [128], BF16)
        S.memset("dve", zeros[:], 0.0)
        S.memset("pool", qdT[:], 0.0)
        S.memset("pool", vtm[:], 0.0)
        wf = WS.next("Bf")
        wq = WS.next("Bq")
        its = [(pr, tb) for pr in range(2) for tb in range(TT)]

        def prep1(k):
            pr, tb = its[k]
            tok = slice(tb * 512, (tb + 1) * 512)
            sg, ff, bb, eb, enb, qf = t2s[k % 2]
            bf_ = zbank()
            fm_matmul(bf_[:, :], wf, pr * 128, lambda kc: hT[:, kc, tok])
            bq_ = zbank()
            fm_matmul(bq_[:, :], wq, pr * 128, lambda kc: hT[:, kc, tok])
            S.act(sg[:], bf_[:, :], AF.Sigmoid)
            S.act(qf[:], bq_[:, :], AF.Silu)
            S.ts("dve", ff[:], sg[:], lbt[:, l, pr, 1:2], lbt[:, l, pr, 0:1], ALU.mult, ALU.add)

        def prep2(k):
            pr, tb = its[k]
            tok = slice(tb * 512, (tb + 1) * 512)
            sg, ff, bb, eb, enb, qf = t2s[k % 2]
            S.act(bb[:], ff[:], AF.Ln)
            S.op("dve", lambda e: e.tensor_tensor_scan(out=sg[:].ap, data0=cm[:, 1, :].ap, data1=bb[:].ap, initial=0.0,
                                                       op0=ALU.mult, op1=ALU.add),
                 reads=[cm[:, 1, :], bb[:]], writes=[sg[:]])
            S.ts("dve", sg[:], sg[:], -80.0, None, ALU.max)
            S.act(eb[:], sg[:], AF.Exp)
            S.act(enb[:], sg[:], AF.Exp, scale=-1.0)
            S.tt("dve", qdT[0:64, 2 * pr, tok], qf[0:64, :], eb[0:64, :], ALU.mult)
            S.tt("dve", qdT[64:128, 2 * pr + 1, tok], qf[64:128, :], eb[64:128, :], ALU.mult)
            S.ts("dve", ff[:], ff[:], -1.0, 1.0, ALU.mult, ALU.add)
            S.tt("dve", kdT[:, pr, tok], ff[:], enb[:], ALU.mult)
            ebs = Acc(eb.t[:, 63:512:64], S.arena, eb[:].slots)
            S.copy("dve", decs[:, pr, tb * 8:(tb + 1) * 8], ebs)

        for k in range(len(its)):
            prep1(k)
            prep2(k)
        wi_ = WS.next("Bi")
        for i in range(NT):
            bank = zbank()
            for kc in range(NKC):
                S.mm(bank[:, 0:256], hT[:, kc, i * 128:(i + 1) * 128], wi_[:, kc, :], start=(kc == 0), stop=(kc == NKC - 1))
            S.copy("act", vtm[0:64, i, 0, :], bank[0:64, 0:256])
            S.copy("dve", vtm[64:128, i, 1, :], bank[64:128, 0:256])
        wg = WS.next("Bg")
        for pr in range(2):
            for tb in range(TT):
                tok = slice(tb * 512, (tb + 1) * 512)
                bank = zbank()
                fm_matmul(bank[:, :], wg, pr * 128, lambda kc: hT[:, kc, tok])
                S.act(sgate[:, pr, tok], bank[:, :], AF.Silu)
        for i in range(NT):
            tb_ = PS[3 + i % 2]
            for pr in range(2):
                S.transpose(tb_.bf((slice(None), slice(pr * 128, (pr + 1) * 128))), kdT[:, pr, i * 128:(i + 1) * 128], ident[:])
            S.copy("act" if i % 2 else "dve", kdtok[:, i, :], tb_.bf((slice(None), slice(0, 256))))
        def coreA(i):
            tok = slice(i * 128, (i + 1) * 128)
            sb = PS[i % 2]
            for h in range(4):
                S.mm(sb[:, h * 128:(h + 1) * 128], kdT[:, h // 2, tok], qdT[:, h, tok])
            db = PS[4 + i % 2]
            for cc in range(2):
                for h in range(4):
                    off = 64 * (h % 2)
                    pr = h // 2
                    S.mm(db[off:off + 64, (cc * 2 + pr) * 64:(cc * 2 + pr + 1) * 64],
                         kdtok[:, i, h * 64:(h + 1) * 64],
                         vtm[:, i, cc, h * 64:(h + 1) * 64])
            am = Am[i % 2]
            S.tt("dve", am[:], sb[:, :], cm[:, 0, :], ALU.mult)
            for cc in range(2):
                n = 2 * i + cc
                for pr in range(2):
                    dl = db[:, (cc * 2 + pr) * 64:(cc * 2 + pr + 1) * 64]
                    if n == 0:
                        S.copy("dve", Ust[pr][n % 2][:], dl)
                    else:
                        S.stt("dve", Ust[pr][n % 2][:], Ust[pr][(n + 1) % 2][:], decs[:, pr, n - 1:n], dl, ALU.mult, ALU.add)
                    if n + 1 < NCH:
                        S.act(Sbf[pr][(n + 1) % 5][:], Ust[pr][n % 2][:], AF.Copy, scale=decs[:, pr, n:n + 1])

        def coreB(i):
            tok = slice(i * 128, (i + 1) * 128)
            am = Am[i % 2]
            ob = PS[2 + i % 2]
            S.mm(ob[:, 0:256], zeros[:], am[:, 0:256], start=True, stop=False, skip=True)
            for h in range(4):
                off = 64 * (h % 2)
                pr = h // 2
                for cc in range(2):
                    S.mm(ob[off:off + 64, pr * 128:(pr + 1) * 128], vtm[:, i, cc, h * 64:(h + 1) * 64], am[:, h * 128:(h + 1) * 128],
                         start=False, stop=False, skip=True)
            for cc in range(2):
                n = 2 * i + cc
                if n == 0:
                    continue
                for h in range(4):
                    off = 64 * (h % 2)
                    pr = h // 2
                    S.mm(ob[off:off + 64, pr * 128 + cc * 64:pr * 128 + (cc + 1) * 64],
                         Sbf[pr][n % 5][:],
                         qdT[:, h, i * 128 + cc * 64:i * 128 + (cc + 1) * 64], start=False, stop=False, skip=True)
            orw = oraw[i % 2]
            S.copy("act", orw[:], ob[:, 0:256])
            S.act(osq[i % 2][:], orw[:], AF.Square)
            nb = PS[6 + i % 2]
            S.mm(nb[:, 0:256], bones[:], osq[i % 2][:])
            lnv = t[0]
            rs = t[1]
            rstd_from_ss(rs[:, 0:256], nb[:, 0:256], 64, lnv[:, 0:256])
            for pr in range(2):
                S.stt("dve", orw[:, pr * 128:(pr + 1) * 128], orw[:, pr * 128:(pr + 1) * 128], pcol[:, l, 32 + pr:33 + pr],
                      rs[:, pr * 128:(pr + 1) * 128], ALU.mult, ALU.mult)
            o2 = Acc(orw.t[:].rearrange("p (a t) -> p a t", a=2), S.arena, orw[:].slots)
            S.tt("dve", o_all[:, 1, :, tok], o2, sgate[:, :, tok], ALU.mult)

        for i in range(NT + 1):
            if i < NT:
                coreA(i)
            if i >= 1:
                coreB(i - 1)

    def mixer_C(l):
        A = Alloc(S, p1, ARENA_END)
        uT = A("uT", [2, SEQ], BF16)
        vntok = A("vntok", [NT, 256], BF16)
        junk = A("cjunk", [256], F32)
        t2 = [A("ct%d" % i, [256], F32) for i in range(2)]
        S.dma("sp", gnorm[:].ap, gnorm_d[l], writes=[gnorm[:]])
        S.dma("sp", wsTf[:].ap, wsT_d[l], writes=[wsTf[:]])
        S.dma("sp", bst[:].ap, bst_d[l], writes=[bst[:]])
        S.tt("dve", wsTb[:], wsTf[:], Acc(cm.t[:, 2, 0:128].unsqueeze(1).to_broadcast([128, 4, 128]), S.arena, cm[:, 2, 0:128].slots),
             ALU.mult)
        wv = WS.next("Cv")
        vall = A("cvall", [NT, 256], F32)
        css = A("css", [3, NT], F32)
        S.memset("dve", css[:], 0.0)
        for i in range(NT):
            bank = zbank()
            for kc in range(NKC):
                S.mm(bank[:, 0:256], hT[:, kc, i * 128:(i + 1) * 128], wv[:, kc, :], start=(kc == 0), stop=(kc == NKC - 1))
            S.act(vall[:, i, :], bank[:, 0:256], AF.Gelu_apprx_tanh)
            S.act(junk[:], vall[:, i, :], AF.Square, accum=css[:, 0, i:i + 1])
        rstd_from_ss(css[:, 2, :], css[:, 0, :], MIXW, css[:, 1, :])
        for i in range(NT):
            S.stt("dve", vntok[:, i, :], vall[:, i, :], css[:, 2, i:i + 1], gnorm[:], ALU.mult, ALU.mult)
        wu = WS.next("Cu")
        for pr in range(2):
            for tb in range(TT):
                tok = slice(tb * 512, (tb + 1) * 512)
                bank = zbank()
                fm_matmul(bank[:, :], wu, pr * 128, lambda kc: hT[:, kc, tok])
                S.act(uT[:, pr, tok], bank[:, :], AF.Gelu_apprx_tanh)
        for i in range(NT):
            tok = slice(i * 128, (i + 1) * 128)
            mb = PS[3 + i % 2]
            for g in range(4):
                off = 64 * (g % 2)
                pr = g // 2
                S.mm(mb[off:off + 64, pr * 128:(pr + 1) * 128], vntok[:, i, g * 64:(g + 1) * 64], wsTb[:, g, :])
            tm = t2[i % 2]
            tm3 = Acc(tm.t[:].rearrange("p (a t) -> p a t", a=2), S.arena, tm[:].slots)
            mb3 = Acc(mb.t[:, 0:256].rearrange("p (a t) -> p a t", a=2), mb.space, (0,))
            S.tt("dve", tm3, mb3, bst[:], ALU.add)
            S.tt("dve", o_all[:, 2, :, tok], tm3, uT[:, :, tok], ALU.mult)

    def mixer_D(l):
        A = Alloc(S, p1, ARENA_END)
        xraw = A("xraw", [2, SEQ], F32)
        ggate = A("ggate", [2, SEQ], BF16)
        t2s = [[A("dt%d_%d" % (k, i), [512], F32) for i in range(8)] for k in range(2)]
        xcb = [A("dxcb%d" % i, [512], BF16) for i in range(2)]
        S.dma("sp", waxf[:].ap, wax_d[l], writes=[waxf[:]])
        S.copy("dve", waxb[:], waxf[:])
        wg = WS.next("Dg")
        for pr in range(2):
            for tb in range(TT):
                tok = slice(tb * 512, (tb + 1) * 512)
                bank = zbank()
                fm_matmul(bank[:, :], wg, pr * 128, lambda kc: hT[:, kc, tok])
                S.act(ggate[:, pr, tok], bank[:, :], AF.Gelu_apprx_tanh)
        wx = WS.next("Dx")
        for pr in range(2):
            for tb in range(TT):
                tok = slice(tb * 512, (tb + 1) * 512)
                bank = zbank()
                fm_matmul(bank[:, :], wx, pr * 128, lambda kc: hT[:, kc, tok])
                S.copy("act" if tb % 2 else "dve", xraw[:, pr, tok], bank[:, :])
        hh2 = [[A("dhp%d_%d" % (p_, i), [512], F32) for i in range(2)] for p_ in range(2)]
        its = [(pr, tb) for tb in range(TT) for pr in range(2)]

        def d1(k):
            pr, tb = its[k]
            t0 = tb * 512
            cw = lambda j: pcol[:, l, 36 + pr * 4 + j:37 + pr * 4 + j]
            xc, r_, ig, a_, a2, ml, bt_, _ = t2s[k % 2]
            S.ts("dve", xc[:], xraw[:, pr, t0:t0 + 512], cw(3), pcol[:, l, 34 + pr:35 + pr], ALU.mult, ALU.add)
            for j in range(3):
                sh = 3 - j
                lo = sh if tb == 0 else 0
                S.stt("dve", xc[:, lo:512], xraw[:, pr, t0 + lo - sh:t0 + 512 - sh], cw(j), xc[:, lo:512], ALU.mult, ALU.add)
            xb = xcb[k % 2]
            S.copy("act", xb[:], xc[:])
            ba_ = PS[3 + k % 2]
            bx_ = PS[5 + k % 2]
            S.mm(ba_[:, :], waxb[:, 0, pr, :], xb[:])
            S.mm(bx_[:, :], waxb[:, 1, pr, :], xb[:])
            S.act(r_[:], ba_[:, :], AF.Sigmoid, bias=pcol[:, l, 44 + pr:45 + pr])
            S.act(ig[:], bx_[:, :], AF.Sigmoid, bias=pcol[:, l, 46 + pr:47 + pr])

        def d2(k):
            pr, tb = its[k]
            t0 = tb * 512
            xc, r_, ig, a_, a2, ml, bt_, _ = t2s[k % 2]
            S.act(a_[:], r_[:], AF.Exp, scale=lruc[:, l, pr, 0:1])
            S.act(a2[:], r_[:], AF.Exp, scale=lruc[:, l, pr, 1:2])
            S.ts("dve", a2[:], a2[:], -1.0, 1.0, ALU.mult, ALU.add)
            S.act(ml[:], a2[:], AF.Sqrt)
            if tb == 0:
                S.memset("dve", ml[:, 0:1], 1.0)
            S.tt("dve", ig[:], ig[:], xc[:], ALU.mult)
            S.tt("dve", bt_[:], ml[:], ig[:], ALU.mult)
            h_ = hh2[pr][tb % 2]
            hp = hh2[pr][(tb + 1) % 2]
            if tb == 0:
                S.op("dve", lambda e: e.tensor_tensor_scan(out=h_[:].ap, data0=a_[:].ap, data1=bt_[:].ap, initial=0.0,
                                                           op0=ALU.mult, op1=ALU.add),
                     reads=[a_[:], bt_[:]], writes=[h_[:]])
            else:
                S.op("dve", lambda e: e.tensor_tensor_scan(out=h_[:].ap, data0=a_[:].ap, data1=bt_[:].ap,
                                                           initial=hp[:, 511:512].ap, op0=ALU.mult, op1=ALU.add),
                     reads=[a_[:], bt_[:], hp[:, 511:512]], writes=[h_[:]])
            S.tt("dve", o_all[:, 3, pr, t0:t0 + 512], h_[:], ggate[:, pr, t0:t0 + 512], ALU.mult)

        for k in range(len(its) + 1):
            if k < len(its):
                d1(k)
            if k >= 1:
                d2(k - 1)

    def block_phase(l, tb, last_layer):
        A = Alloc(S, p1, ARENA_END)
        tok = slice(tb * 512, (tb + 1) * 512)
        xt = A("xt", [NKC, 512], F32)
        yT = A("yT", [NKC, 512], F32)
        macc = A("macc", [NKC, 512], F32)
        mrg = A("mrg", [NKC, 512], BF16)
        hid = View(S, "hid", [NJ, 512], BF16, macc.off)
        sg = [A("sg%d" % i, [512], F32) for i in range(2)]
        tp = [A("tp%d" % i, [512], F32) for i in range(2)]
        A2 = Alloc(S, A.cur, ARENA_END)
        S.dma("sp", xt[:].ap, xT_d[:, :, tok], reads=[Acc(None, xT_space, (tb,))], writes=[xt[:]])
        k = 0
        for n in range(4):
            for q4 in range(4):
                wgt = WS.next("G")
                wb = WS.next("WB")
                for d2 in range(2):
                    dc = q4 * 2 + d2
                    gb = zbank()
                    fm_matmul(gb[:, :], wgt, d2 * 128, lambda kc: hT[:, kc, tok])
                    pb = zbank()
                    for kc in range(2):
                        S.mm(pb[:, :], wb[:, kc, d2 * 128:(d2 + 1) * 128], o_all[:, n, kc, tok], start=(kc == 0), stop=(kc == 1))
                    s_ = sg[k % 2]
                    S.act(s_[:], gb[:, :], AF.Sigmoid)
                    if n == 0:
                        S.tt("dve", macc[:, dc, :], s_[:], pb[:, :], ALU.mult)
                    else:
                        t_ = tp[k % 2]
                        S.tt("dve", t_[:], s_[:], pb[:, :], ALU.mult)
                        if n < 3:
                            S.tt("dve", macc[:, dc, :], macc[:, dc, :], t_[:], ALU.add)
                        else:
                            S.tt("dve", mrg[:, dc, :], macc[:, dc, :], t_[:], ALU.add)
                    k += 1
        ev = Evac(yT, 8, l, A2)
        for q4 in range(4):
            wo = WS.next("WO")
            for d2 in range(2):
                dc = q4 * 2 + d2
                yb = zbank()
                fm_matmul(yb[:, :], wo, d2 * 128, lambda kc: mrg[:, kc, :])
                ev.chunk(dc, yb)
        ev.finish()
        resid_norm(xt, yT, l, 16, tb, A2, ev.sq)
        for j2 in range(NJ // 2):
            w1g = WS.next("F1g")
            w1u = WS.next("F1u")
            for jj in range(2):
                j = j2 * 2 + jj
                gb = zbank()
                fm_matmul(gb[:, :], w1g, jj * 128, lambda kc: hT[:, kc, tok])
                ub = zbank()
                fm_matmul(ub[:, :], w1u, jj * 128, lambda kc: hT[:, kc, tok])
                s_ = sg[j % 2]
                S.act(s_[:], gb[:, :], AF.Silu)
                S.tt("dve", hid[:, j, :], s_[:], ub[:, :], ALU.mult)
        A2.cur = A.cur
        ev = Evac(yT, 24, l, A2)
        for dc in range(NKC):
            wa_ = WS.next("F2")
            wb_ = WS.next("F2")
            yb = zbank()
            fm_matmul(yb[:, :], wa_, 0, lambda kc: hid[:, kc, :], nk=11, k0=0, first=True, last=False)
            fm_matmul(yb[:, :], wb_, 0, lambda kc: hid[:, kc, :], nk=11, k0=11, first=False, last=True)
            ev.chunk(dc, yb)
        ev.finish()
        resid_norm(xt, yT, None if last_layer else l + 1, 0, tb, A2, ev.sq)
        if not last_layer:
            S.dma("sp", xT_d[:, :, tok], xt[:].ap, reads=[xt[:]], writes=[Acc(None, xT_space, (tb,))])
        else:
            A2.cur = A.cur
            otk = [A2("otk%d" % i, [D], F32) for i in range(2)]
            for i4 in range(4):
                ok = otk[i4 % 2]
                for half in range(2):
                    bank = PS[3 + (2 * i4 + half) % 2]
                    for c4 in range(4):
                        c = half * 4 + c4
                        S.transpose(bank[:, c4 * 128:(c4 + 1) * 128], xt[:, c, i4 * 128:(i4 + 1) * 128], identf[:])
                    S.copy("act" if half else "dve", ok[:, half * 512:(half + 1) * 512], bank[:, :])
                r0 = tb * 512 + i4 * 128
                S.dma("sp", out_d[r0:r0 + 128, :], ok[:].ap, reads=[ok[:]], is_output=True)

    def emit():
        if stop == "const":
            return
        load_input()
        if stop == "load":
            return
        for l in range(DEPTH):
            for nm, fn in (("A", mixer_A), ("B", mixer_B), ("C", mixer_C), ("D", mixer_D)):
                fn(l)
                if stop is not None and stop.startswith(nm):
                    return
            if dbg and l == 0:
                S.dma("sp", dbg_d, o_all[:].ap, reads=[o_all[:]], is_output=True)
            for tb in range(TT):
                block_phase(l, tb, l == DEPTH - 1)
    emit()
    if stop is not None and dbg:
        S.dma("sp", dbg_d, o_all[:].ap, reads=[o_all[:]], is_output=True)
    S.finish()
    return nc, S


def prep_shared(inp, DEPTH):
    f = lambda k: np.ascontiguousarray(np.asarray(inp[k], dtype=np.float32))
    pcol = np.zeros((DEPTH, 128, NPCOL), np.float32)

    def fm8(v):
        return v.reshape(DEPTH, 8, 128).transpose(0, 2, 1)

    def fm2(v):
        return v.reshape(DEPTH, 2, 128).transpose(0, 2, 1)

    pcol[:, :, 0:8] = fm8(f("norm_mix_pre"))
    pcol[:, :, 8:16] = fm8(f("norm_mix_post"))
    pcol[:, :, 16:24] = fm8(f("norm_ffn_pre"))
    pcol[:, :, 24:32] = fm8(f("norm_ffn_post"))
    pcol[:, :, 32:34] = fm2(f("hgrn_norm_g"))
    pcol[:, :, 34:36] = fm2(f("lru_conv_b"))
    cw = f("lru_conv_w")
    for pr in range(2):
        for j in range(4):
            pcol[:, :, 36 + pr * 4 + j] = cw[:, j, pr * 128:(pr + 1) * 128]
    pcol[:, :, 44:46] = fm2(f("lru_ba"))
    pcol[:, :, 46:48] = fm2(f("lru_bx"))
    pcol[:, :, 48:50] = fm2(f("lru_lambda"))
    lg = f("hgrn_lb_logits")
    pcol[:, :, 50:58] = NEG
    for pr in range(2):
        for d in range(DEPTH):
            pcol[:, :, 50 + pr * 4 + d] = lg[d, pr * 128:(pr + 1) * 128][None, :]
    gnorm_bc = np.ascontiguousarray(np.broadcast_to(f("gmlp_norm_g")[:, None, :], (DEPTH, 128, MIXW)))
    wsT = np.ascontiguousarray(f("gmlp_ws").transpose(0, 3, 1, 2))
    bs = f("gmlp_bs")
    bs_t = np.zeros((DEPTH, 128, 2, 128), np.float32)
    for g in range(4):
        bs_t[:, (g % 2) * 64:(g % 2) * 64 + 64, g // 2, :] = bs[:, g, None, :]
    wax = np.zeros((DEPTH, 128, 2, 2, 128), np.float32)
    for ai, key in enumerate(("lru_wa", "lru_wx")):
        w = f(key)
        for h in range(4):
            o = (h % 2) * 64
            wax[:, o:o + 64, ai, h // 2, o:o + 64] = w[:, h]
    rb = f("attn_rel_bias")
    k = np.arange(128)[:, None]
    q = np.arange(128)[None, :]
    abias = np.zeros((DEPTH, 128, 5, 4, 128), np.float32)
    for j in range(5):
        dist = (4 - j) * 128 + q - k
        idx = np.clip(dist, -63, 256) + 63
        cq = 8 + q // 64
        ck = 2 * j + k // 64
        valid = (ck >= cq - 8) & (ck <= cq)
        g = rb[:, :, idx]
        g = np.where(valid[None, None], g, np.float32(NEG))
        abias[:, :, j] = g.transpose(0, 2, 1, 3)
    cm = np.zeros((128, 4, 512), np.float32)
    s = np.arange(128)[:, None]
    t = np.arange(128)[None, :]
    m0 = ((s // 64) == (t // 64)) & (s <= t)
    cm[:, 0, :] = np.tile(m0.astype(np.float32), (1, 4))
    cm[:, 1, :] = (np.arange(512) % 64 != 0).astype(np.float32)[None, :]
    cm[:, 2, 0:128] = (s <= t).astype(np.float32)
    cm[:, 3, 0:128] = np.eye(128, dtype=np.float32)
    return {
        "w_in": f("w_in"), "w_branch": f("w_branch"), "w_out": f("w_out"),
        "w_ffn_in": f("w_ffn_in"), "w_ffn_out": f("w_ffn_out"),
        "pcol": pcol, "gnorm_bc": gnorm_bc, "wsT": wsT, "bs_t": bs_t, "wax": wax,
        "abias": abias, "cmasks": cm,
    }


_CACHE = {}


def kernel(**inputs):
    x = np.asarray(inputs["x"], dtype=np.float32)
    B, SEQ, _ = x.shape
    DEPTH = inputs["w_in"].shape[0]
    key = (SEQ, DEPTH)
    if key not in _CACHE:
        _CACHE[key] = build_program(SEQ, DEPTH)[0]
    nc = _CACHE[key]
    shared = prep_shared(inputs, DEPTH)
    in_maps = []
    for b in range(B):
        m = dict(shared)
        m["x"] = np.ascontiguousarray(x[b])
        in_maps.append(m)
    res = run_bass_kernel_spmd(nc, in_maps, core_ids=list(range(B)))
    return np.stack([np.asarray(r["out"], dtype=np.float32) for r in res.results], axis=0)
```
